# Optimizing a Trainium2 kernel written in Bass

```python
import math
import jax, jax.numpy as jnp
from jax import lax
import numpy as np

D_MODEL = 2048
BATCH = 4
SEQ = 2048
DEPTH = 2
DEC_BATCH = 128
DEC_SEQ = 1
PAST_LEN = 8192
PAGE_SIZE = 128

N_MIXERS = 2
N_HEADS = 16
Q_LORA = 512
KV_LORA = 512
QK_NOPE = 128
QK_ROPE = 64
V_DIM = 128
ATTN_SCALE = 1.0 / math.sqrt(QK_NOPE + QK_ROPE)
ROPE_BASE = 10000.0
CONV_W = 31
D_FF = 5632
FFN_CONV_W = 3
Q_BLOCK = 128
EPS = 1e-6
N_MLA = (DEPTH + 1) // 2
N_CONV = DEPTH // 2

kernel_name = "mla_conformer_convffn_adaln_step"


def rmsnorm(x, g):
    xf = x.astype(jnp.float32)
    y = xf * lax.rsqrt(jnp.mean(xf * xf, axis=-1, keepdims=True) + EPS)
    return (y * g.astype(jnp.float32)).astype(x.dtype)


def layernorm(x, g, b):
    xf = x.astype(jnp.float32)
    mu = jnp.mean(xf, axis=-1, keepdims=True)
    var = jnp.mean(jnp.square(xf - mu), axis=-1, keepdims=True)
    y = (xf - mu) * lax.rsqrt(var + EPS) * g.astype(jnp.float32) + b.astype(jnp.float32)
    return y.astype(x.dtype)


def rope(x, pos):
    half = QK_ROPE // 2
    inv = ROPE_BASE ** (-jnp.arange(half, dtype=jnp.float32) / half)
    ang = pos.astype(jnp.float32)[:, None] * inv[None, :]
    shape = (1, ang.shape[0]) + (1,) * (x.ndim - 3) + (half,)
    cos, sin = jnp.cos(ang).reshape(shape), jnp.sin(ang).reshape(shape)
    xf = x.astype(jnp.float32)
    x1, x2 = xf[..., :half], xf[..., half:]
    return jnp.concatenate([x1 * cos - x2 * sin, x1 * sin + x2 * cos], axis=-1).astype(x.dtype)


def causal_dwconv(u_ext, w, b):
    C = u_ext.shape[-1]
    y = lax.conv_general_dilated(u_ext, w[:, None, :].astype(u_ext.dtype), window_strides=(1,),
                                 padding='VALID', dimension_numbers=('NWC', 'WIO', 'NWC'),
                                 feature_group_count=C)
    return y + b


def modulation(c, w_mod, b_mod):
    m = jax.nn.silu(c) @ w_mod + b_mod
    return jnp.split(m[:, None, :], 6, axis=-1)


def mla_project(h, pos, wq_a, q_norm_g, wq_b, wkv_a, kv_norm_g):
    B, T, _ = h.shape
    q = (rmsnorm(h @ wq_a, q_norm_g) @ wq_b).reshape(B, T, N_HEADS, QK_NOPE + QK_ROPE)
    q_nope, q_pe = q[..., :QK_NOPE], rope(q[..., QK_NOPE:], pos)
    kv = h @ wkv_a
    ckv = rmsnorm(kv[..., :KV_LORA], kv_norm_g)
    kpe = rope(kv[..., KV_LORA:], pos)
    return q_nope, q_pe, ckv, kpe


def mla_prompt(q_nope, q_pe, ckv, kpe, w_uk, w_uv, w_o):
    B, S = ckv.shape[0], ckv.shape[1]
    k_nope = jnp.einsum('bsc,chd->bshd', ckv, w_uk)
    v = jnp.einsum('bsc,chd->bshd', ckv, w_uv)
    nb = S // Q_BLOCK
    qn_b = q_nope.reshape(B, nb, Q_BLOCK, N_HEADS, QK_NOPE).transpose(1, 0, 2, 3, 4)
    qp_b = q_pe.reshape(B, nb, Q_BLOCK, N_HEADS, QK_ROPE).transpose(1, 0, 2, 3, 4)
    k_pos = jnp.arange(S)

    def block(args):
        qn, qp, i = args
        s = (jnp.einsum('bqhd,bkhd->bhqk', qn, k_nope)
             + jnp.einsum('bqhr,bkr->bhqk', qp, kpe)).astype(jnp.float32) * ATTN_SCALE
        q_pos = i * Q_BLOCK + jnp.arange(Q_BLOCK)
        s = jnp.where(k_pos[None, :] <= q_pos[:, None], s, -jnp.inf)
        p = jax.nn.softmax(s, axis=-1).astype(v.dtype)
        return jnp.einsum('bhqk,bkhd->bqhd', p, v)

    o = lax.map(block, (qn_b, qp_b, jnp.arange(nb)))
    o = o.transpose(1, 0, 2, 3, 4).reshape(B, S, N_HEADS * V_DIM)
    return o @ w_o


def mla_sample(q_nope, q_pe, ckv_new, kpe_new, cache_ckv, cache_kpe, j, page_table, w_uk, w_uv, w_o):
    Bd, T = ckv_new.shape[0], ckv_new.shape[1]
    ckv_past = cache_ckv[j, page_table].reshape(Bd, -1, KV_LORA)
    kpe_past = cache_kpe[j, page_table].reshape(Bd, -1, QK_ROPE)
    P = ckv_past.shape[1]
    q_lat = jnp.einsum('bthd,chd->bthc', q_nope, w_uk)
    s_past = (jnp.einsum('bthc,bsc->bhts', q_lat, ckv_past)
              + jnp.einsum('bthr,bsr->bhts', q_pe, kpe_past)).astype(jnp.float32)
    s_new = (jnp.einsum('bthc,bsc->bhts', q_lat, ckv_new)
             + jnp.einsum('bthr,bsr->bhts', q_pe, kpe_new)).astype(jnp.float32)
    causal = jnp.arange(T)[None, :] <= jnp.arange(T)[:, None]
    s_new = jnp.where(causal, s_new, -jnp.inf)
    s = jnp.concatenate([s_past, s_new], axis=-1) * ATTN_SCALE
    p = jax.nn.softmax(s, axis=-1).astype(ckv_new.dtype)
    o_lat = (jnp.einsum('bhts,bsc->bthc', p[..., :P], ckv_past)
             + jnp.einsum('bhts,bsc->bthc', p[..., P:], ckv_new))
    o = jnp.einsum('bthc,chd->bthd', o_lat, w_uv).reshape(Bd, T, N_HEADS * V_DIM)
    return o @ w_o


def conv_module(h, state, w_pw1, b_pw1, w_dw, b_dw, ln_g, ln_b, w_pw2, b_pw2):
    u = h @ w_pw1 + b_pw1
    u = u[..., :D_MODEL] * jax.nn.sigmoid(u[..., D_MODEL:])
    u_ext = jnp.concatenate([state, u], axis=1)
    y = jax.nn.silu(layernorm(causal_dwconv(u_ext, w_dw, b_dw), ln_g, ln_b))
    return y @ w_pw2 + b_pw2, u_ext[:, -(CONV_W - 1):]


def conv_ffn(h, state, w_gate, w_up, w_conv, b_conv, w_down):
    g = h @ w_gate
    u = h @ w_up
    g_ext = jnp.concatenate([state, g], axis=1)
    g = causal_dwconv(g_ext, w_conv, b_conv)
    return (jax.nn.silu(g) * u) @ w_down, g_ext[:, -(FFN_CONV_W - 1):]


def trunk(x, c, W, cache_ckv=None, cache_kpe=None, page_table=None, state_conv=None, state_ffn=None):
    sample = page_table is not None
    B, T, _ = x.shape
    if sample:
        past_len = page_table.shape[1] * cache_ckv.shape[2]
        pos = past_len + jnp.arange(T, dtype=jnp.int32)
    else:
        pos = jnp.arange(T, dtype=jnp.int32)
    new_ckv, new_kpe, new_conv, new_ffn = [], [], [], []
    for i in range(DEPTH):
        sh_a, sc_a, g_a, sh_f, sc_f, g_f = modulation(c, W['w_mod'][i], W['b_mod'][i])
        h = rmsnorm(x, W['norm_mix_g'][i]) * (1 + sc_a) + sh_a
        j = i // N_MIXERS
        if i % N_MIXERS == 0:
            q_nope, q_pe, ckv, kpe = mla_project(h, pos, W['wq_a'][j], W['q_norm_g'][j], W['wq_b'][j],
                                                 W['wkv_a'][j], W['kv_norm_g'][j])
            if sample:
                out = mla_sample(q_nope, q_pe, ckv, kpe, cache_ckv, cache_kpe, j, page_table,
                                 W['w_uk'][j], W['w_uv'][j], W['w_o'][j])
            else:
                out = mla_prompt(q_nope, q_pe, ckv, kpe, W['w_uk'][j], W['w_uv'][j], W['w_o'][j])
            new_ckv.append(ckv)
            new_kpe.append(kpe)
        else:
            st = state_conv[j] if sample else jnp.zeros((B, CONV_W - 1, D_MODEL), x.dtype)
            out, st_new = conv_module(h, st, W['conv_w_pw1'][j], W['conv_b_pw1'][j], W['conv_w_dw'][j],
                                      W['conv_b_dw'][j], W['conv_ln_g'][j], W['conv_ln_b'][j],
                                      W['conv_w_pw2'][j], W['conv_b_pw2'][j])
            new_conv.append(st_new)
        x = x + g_a * out
        h = rmsnorm(x, W['norm_ffn_g'][i]) * (1 + sc_f) + sh_f
        st = state_ffn[i] if sample else jnp.zeros((B, FFN_CONV_W - 1, D_FF), x.dtype)
        out, st_new = conv_ffn(h, st, W['ffn_w_gate'][i], W['ffn_w_up'][i], W['ffn_w_conv'][i],
                               W['ffn_b_conv'][i], W['ffn_w_down'][i])
        new_ffn.append(st_new)
        x = x + g_f * out
    y = rmsnorm(x, W['final_norm_g'])
    return y, jnp.stack(new_ckv), jnp.stack(new_kpe), jnp.stack(new_conv), jnp.stack(new_ffn)


def setup_inputs(seed: int = 0) -> dict:
    key = jax.random.key(seed)
    keys = jax.random.split(key, 48)
    counter = [0]

    def nk():
        k = keys[counter[0]]
        counter[0] += 1
        return k

    def nrm(shape, scale):
        return jax.random.normal(nk(), shape, jnp.float32) * scale

    def gain(shape):
        return 1.0 + nrm(shape, 0.05)

    n_pages = PAST_LEN // PAGE_SIZE
    n_used = DEC_BATCH * n_pages
    n_pool = n_used + max(1, n_used // 4)
    page_table = jax.random.permutation(nk(), n_pool)[:n_used].reshape(DEC_BATCH, n_pages).astype(jnp.int32)
    D = D_MODEL
    HV = N_HEADS * V_DIM
    return {
        'x_prompt': nrm((BATCH, SEQ, D), 1.0),
        'x_sample': nrm((DEC_BATCH, DEC_SEQ, D), 1.0),
        'cache_ckv': nrm((N_MLA, n_pool, PAGE_SIZE, KV_LORA), 1.0),
        'cache_kpe': nrm((N_MLA, n_pool, PAGE_SIZE, QK_ROPE), 1.0),
        'state_conv': nrm((N_CONV, DEC_BATCH, CONV_W - 1, D), 0.5),
        'state_ffn': nrm((DEPTH, DEC_BATCH, FFN_CONV_W - 1, D_FF), 0.5),
        'page_table': page_table,
        'c_prompt': nrm((BATCH, D), 1.0),
        'c_sample': nrm((DEC_BATCH, D), 1.0),
        'w_mod': nrm((DEPTH, D, 6 * D), 0.5 * D ** -0.5),
        'b_mod': nrm((DEPTH, 6 * D), 0.01),
        'norm_mix_g': gain((DEPTH, D)),
        'norm_ffn_g': gain((DEPTH, D)),
        'wq_a': nrm((N_MLA, D, Q_LORA), D ** -0.5),
        'q_norm_g': gain((N_MLA, Q_LORA)),
        'wq_b': nrm((N_MLA, Q_LORA, N_HEADS * (QK_NOPE + QK_ROPE)), Q_LORA ** -0.5),
        'wkv_a': nrm((N_MLA, D, KV_LORA + QK_ROPE), D ** -0.5),
        'kv_norm_g': gain((N_MLA, KV_LORA)),
        'w_uk': nrm((N_MLA, KV_LORA, N_HEADS, QK_NOPE), KV_LORA ** -0.5),
        'w_uv': nrm((N_MLA, KV_LORA, N_HEADS, V_DIM), KV_LORA ** -0.5),
        'w_o': nrm((N_MLA, HV, D), HV ** -0.5),
        'conv_w_pw1': nrm((N_CONV, D, 2 * D), D ** -0.5),
        'conv_b_pw1': nrm((N_CONV, 2 * D), 0.01),
        'conv_w_dw': nrm((N_CONV, CONV_W, D), CONV_W ** -0.5),
        'conv_b_dw': nrm((N_CONV, D), 0.01),
        'conv_ln_g': gain((N_CONV, D)),
        'conv_ln_b': nrm((N_CONV, D), 0.01),
        'conv_w_pw2': nrm((N_CONV, D, D), D ** -0.5),
        'conv_b_pw2': nrm((N_CONV, D), 0.01),
        'ffn_w_gate': nrm((DEPTH, D, D_FF), D ** -0.5),
        'ffn_w_up': nrm((DEPTH, D, D_FF), D ** -0.5),
        'ffn_w_conv': nrm((DEPTH, FFN_CONV_W, D_FF), FFN_CONV_W ** -0.5),
        'ffn_b_conv': nrm((DEPTH, D_FF), 0.01),
        'ffn_w_down': nrm((DEPTH, D_FF, D), D_FF ** -0.5),
        'final_norm_g': gain((D,)),
    }


def reference(x_prompt, x_sample, cache_ckv, cache_kpe, state_conv, state_ffn, page_table, c_prompt, c_sample,
              w_mod, b_mod, norm_mix_g, norm_ffn_g, wq_a, q_norm_g, wq_b, wkv_a, kv_norm_g, w_uk, w_uv, w_o,
              conv_w_pw1, conv_b_pw1, conv_w_dw, conv_b_dw, conv_ln_g, conv_ln_b, conv_w_pw2, conv_b_pw2,
              ffn_w_gate, ffn_w_up, ffn_w_conv, ffn_b_conv, ffn_w_down, final_norm_g):
    W = dict(w_mod=w_mod, b_mod=b_mod, norm_mix_g=norm_mix_g, norm_ffn_g=norm_ffn_g, wq_a=wq_a,
             q_norm_g=q_norm_g, wq_b=wq_b, wkv_a=wkv_a, kv_norm_g=kv_norm_g, w_uk=w_uk, w_uv=w_uv, w_o=w_o,
             conv_w_pw1=conv_w_pw1, conv_b_pw1=conv_b_pw1, conv_w_dw=conv_w_dw, conv_b_dw=conv_b_dw,
             conv_ln_g=conv_ln_g, conv_ln_b=conv_ln_b, conv_w_pw2=conv_w_pw2, conv_b_pw2=conv_b_pw2,
             ffn_w_gate=ffn_w_gate, ffn_w_up=ffn_w_up, ffn_w_conv=ffn_w_conv, ffn_b_conv=ffn_b_conv,
             ffn_w_down=ffn_w_down, final_norm_g=final_norm_g)
    y_prompt, p_ckv, p_kpe, p_conv, p_ffn = trunk(x_prompt, c_prompt, W)
    y_sample, s_ckv, s_kpe, s_conv, s_ffn = trunk(x_sample, c_sample, W, cache_ckv=cache_ckv,
                                                  cache_kpe=cache_kpe, page_table=page_table,
                                                  state_conv=state_conv, state_ffn=state_ffn)
    return (y_prompt, y_sample, p_ckv, p_kpe, s_ckv, s_kpe, p_conv, s_conv, p_ffn, s_ffn)
```

```python
import contextlib
import math
import numpy as np
import concourse.bass as bass
import concourse.mybir as mybir
from concourse.bass_utils import run_bass_kernel_spmd

F32 = mybir.dt.float32
BF16 = mybir.dt.bfloat16
I32 = mybir.dt.int32
AF = mybir.ActivationFunctionType
ALU = mybir.AluOpType
AX = mybir.AxisListType

ENGS = ("pe", "act", "dve", "pool", "sp")

D = 2048
KC = 16
HALO = 34
OWN = 1024
NPR = OWN + HALO
NS = 16
T = NPR + NS
FAR = 1024 - HALO
NKEY = 2048
CKW = NKEY + NS
TILES = [(0, 358), (358, 716), (716, 1074)]
CTXT = [(0, 330), (330, 660), (660, 990)]
DFF = 5632
NJ = 44
NH = 16
SCALE = 1.0 / math.sqrt(192.0)
NPOOL = 10240
FGROUPS = [(0, 8), (8, 16), (16, 24), (24, 32), (32, 40), (40, 44)]
QT = [(0, HALO), (HALO, HALO + 512), (HALO + 512, NPR)]

VC = {}
_o = 0
for _n, _w in [("b_mod", 192), ("g_mix", 32), ("g_ffn", 32), ("g_fin", 16), ("g_q", 4), ("g_kv", 4),
               ("b_pw1", 32), ("w_dw", 31 * 16), ("b_dw", 16), ("ln_g", 16), ("ln_b", 16), ("b_pw2", 16),
               ("w_fc", 2 * 44 * 3), ("b_fc", 2 * 44), ("hm", 1), ("cb", 1), ("zero", 1), ("pm16", 1),
               ("eps", 1), ("tiny", 1)]:
    VC[_n] = _o
    _o += _w
NV = _o


class Res:
    __slots__ = ("w", "r")

    def __init__(self):
        self.w = None
        self.r = []


def alias(olds):
    n = Res()
    for o in olds:
        if o.w is not None:
            n.r.append(o.w)
        n.r.extend(o.r)
    return n


class Ins:
    __slots__ = ("eng", "fn", "deps", "sig", "is_dma", "dsem", "dval", "has_dep")

    def __init__(self, eng, fn, is_dma):
        self.eng = eng
        self.fn = fn
        self.deps = []
        self.sig = 0
        self.is_dma = is_dma
        self.dsem = None
        self.dval = 0
        self.has_dep = False


class Sched:
    def __init__(self, nc, n_dma_sems=8):
        self.nc = nc
        self.q = {e: [] for e in ENGS}
        self.n_dma_sems = n_dma_sems

    def _add(self, eng, fn, reads, writes, is_dma):
        ins = Ins(eng, fn, is_dma)
        deps = []
        for r in reads:
            if r.w is not None:
                deps.append(r.w)
        for w in writes:
            if w.w is not None:
                deps.append(w.w)
            deps.extend(w.r)
        seen = set()
        for d in deps:
            if d is ins or id(d) in seen:
                continue
            seen.add(id(d))
            if (not d.is_dma) and d.eng == "pe" and eng == "pe" and not is_dma:
                continue
            ins.deps.append(d)
            d.has_dep = True
        for r in reads:
            r.r.append(ins)
        for w in writes:
            w.w = ins
            w.r = []
        self.q[eng].append(ins)
        return ins

    def op(self, eng, fn, reads=(), writes=()):
        return self._add(eng, fn, reads, writes, False)

    def dma(self, eng, fn, reads=(), writes=()):
        return self._add(eng, fn, reads, writes, True)

    def emit(self):
        nc = self.nc
        with contextlib.ExitStack() as st:
            esem = {e: st.enter_context(nc.semaphore("s_" + e)) for e in ENGS}
            dsem = {e: [st.enter_context(nc.semaphore("d_%s%d" % (e, i))) for i in range(self.n_dma_sems)]
                    for e in ("sp", "pool", "act")}
            for e in ENGS:
                cnt = 0
                dcnt = [0] * self.n_dma_sems
                rr = 0
                for ins in self.q[e]:
                    if ins.is_dma:
                        k = rr % self.n_dma_sems
                        rr += 1
                        dcnt[k] += 16
                        ins.dsem = (e, k)
                        ins.dval = dcnt[k]
                    elif ins.has_dep:
                        cnt += 1
                        ins.sig = cnt
            block = st.enter_context(nc.Block())
            engobj = {"pe": "tensor", "act": "scalar", "dve": "vector", "pool": "gpsimd", "sp": "sync"}

            def run_engine(e, eng):
                known = {}

                def wait(key, sem, val):
                    if known.get(key, 0) >= val:
                        return
                    known[key] = val
                    eng.wait_ge(sem, val)

                last_dma = {}
                for ins in self.q[e]:
                    for d in ins.deps:
                        if d.is_dma:
                            wait(("d",) + d.dsem, dsem[d.dsem[0]][d.dsem[1]], d.dval)
                        else:
                            wait(("e", d.eng), esem[d.eng], d.sig)
                    if ins.is_dma:
                        k = ins.dsem[1]
                        if ins.dval > 16:
                            wait(("d", e, k), dsem[e][k], ins.dval - 16)
                        ins.fn(eng).then_inc(dsem[e][k], 16)
                        last_dma[k] = ins.dval
                    else:
                        r = ins.fn(eng)
                        if ins.sig:
                            r.then_inc(esem[e], 1)
                for k, v in last_dma.items():
                    wait(("d", e, k), dsem[e][k], v)

            for e in ENGS:
                if not self.q[e]:
                    continue

                def body(eng, e=e):
                    run_engine(e, eng)
                getattr(block, engobj[e])(body)


def bcast(ap, axis, n):
    l = [list(x) for x in ap.ap]
    l.insert(axis, [0, n])
    return bass.AP(ap.tensor, ap.offset, l)


def build_nc(stop_after=None):
    nc = bass.Bass("TRN2", target_bir_lowering=False)
    S = Sched(nc)

    def din(name, shape, dt=F32):
        return nc.dram_tensor(name, list(shape), dt, kind="ExternalInput").ap()

    def dout(name, shape, dt=F32):
        return nc.dram_tensor(name, list(shape), dt, kind="ExternalOutput").ap()

    xT = din("xT", [D, NKEY])
    xsT = din("xsT", [D, NS])
    cT = din("cT", [D, 17])
    vecs_d = din("vecs", [128, NV])
    rope_d = din("rope", [64, 2, CKW])
    tri_d = din("tri", [128, 128 + HALO])
    ident_d = din("ident", [128, 128])
    ptrep_d = din("ptrep", [128, 256], I32)
    cckv = din("cache_ckv", [NPOOL * 32, 4 * 512])
    ckpe = din("cache_kpe", [NPOOL * 32, 4 * 64])
    sconv = din("state_conv", [D, NS, 30])
    sffn = din("state_ffn", [2, DFF, NS, 2])
    w_mod = din("w_mod", [2, D, 6 * D])
    wq_a = din("wq_a", [D, 512])
    wq_b = din("wq_b", [512, NH * 256])
    wkv_a = din("wkv_a", [D, 512 + 128])
    w_uk = din("w_uk", [512, NH * 128])
    w_ukT = din("w_ukT", [128, NH * 512])
    w_uv = din("w_uv", [512, NH * 128])
    w_o = din("w_o", [D, D])
    w_pw1 = din("w_pw1", [D, 2 * D])
    w_pw2 = din("w_pw2", [D, D])
    w_gate = din("w_gate", [2, D, DFF])
    w_up = din("w_up", [2, D, DFF])
    w_down = din("w_down", [2, DFF, D])
    yT = dout("yT", [D, OWN])
    ysT = dout("ysT", [D, NS])
    ckv_o = dout("ckv_o", [512, OWN])
    kpe_o = dout("kpe_o", [64, OWN])
    ckv_so = dout("ckv_so", [512, NS])
    kpe_so = dout("kpe_so", [64, NS])
    conv_o = dout("conv_o", [D, 30])
    conv_so = dout("conv_so", [D, NS, 30])
    ffn_o = dout("ffn_o", [2, DFF, 2])
    ffn_so = dout("ffn_so", [2, DFF, NS, 2])

    with contextlib.ExitStack() as st:
        def sb(name, shape, dt):
            return st.enter_context(nc.sbuf_tensor("sb_" + name, list(shape), dt))

        xres = sb("xres", [128, KC, T], F32)
        hbuf = sb("hbuf", [128, KC * T], BF16)
        arA = sb("arA", [128, KC * T], BF16)
        arB = sb("arB", [128, 4 * T], BF16)
        arC = sb("arC", [128, 3328], F32)
        NWB = 2
        wb = [sb("wb%d" % i, [128, 4096], BF16) for i in range(NWB)]
        vecs = sb("vecs", [128, NV], F32)
        modv = sb("modv", [128, 6, KC, 17], F32)
        tri = sb("tri_f", [128, 128 + HALO], F32)
        trib = sb("trib", [128, 128 + HALO], BF16)
        ident = sb("ident", [128, 128], BF16)
        identf = sb("identf", [128, 128], F32)
        ones = sb("ones", [128, 128], BF16)
        csb = sb("csb", [128, KC, 17], F32)
        csbf = sb("csbf", [128, KC, 17], BF16)
        rstd = [sb("rstd%d" % i, [128, 384], F32) for i in range(2)]
        tmpf = [sb("tmpf%d" % i, [128, 384], F32) for i in range(2)]
        gbuf = [sb("gbuf%d" % i, [128, 384], F32) for i in range(2)]
        sfst = sb("sfst", [128, 8, NS, 2], F32)
        pfo = sb("pfo", [128, NJ, 2], F32)
        small = sb("small", [128, 512], F32)
        qs_n = sb("qs_n", [128, NH, NS], BF16)
        qs_p = sb("qs_p", [64, NH, NS], BF16)
        osT = sb("osT", [128, NH, NS], BF16)
        idx_i = sb("idx_i", [128, 256], I32)
        idx_f = sb("idx_f", [128, 256], F32)
        PS = [st.enter_context(nc.psum_tensor("ps%d" % i, [128, 512], F32)) for i in range(8)]
        rPS = [Res() for _ in range(8)]

        r_xres = [Res() for _ in range(3)]
        r_vecs, r_modv, r_tri, r_ident, r_ones, r_csb, r_csbf = (Res() for _ in range(7))
        r_rstd = [Res(), Res()]
        r_tmpf = [Res(), Res()]
        r_gbuf = [Res(), Res()]
        r_wb = [Res() for _ in range(NWB)]
        r_wbB = [Res() for _ in range(NWB)]
        r_sfst, r_pfo, r_small = Res(), Res(), Res()
        r_qs, r_osT, r_idx = Res(), Res(), Res()
        cnt = {"wb": 0, "tmp": 0, "rstd": 0, "ps": 0, "ps45": 0}
        h3 = hbuf[:, :].rearrange("p (c t) -> p c t", c=KC)

        def V(name, off=0, n=1):
            c = VC[name] + off
            return vecs[:, c:c + n]

        def wload(view, shape):
            k = cnt["wb"] % NWB
            cnt["wb"] += 1
            n = int(np.prod(shape[1:]))
            assert n <= 4096
            dst = wb[k][:, 0:n]
            if len(shape) == 3:
                dst = dst.rearrange("p (a b) -> p a b", a=shape[1])
            elif len(shape) == 4:
                dst = dst.rearrange("p (a b c) -> p a b c", a=shape[1], b=shape[2])
            S.dma("pool", lambda e: e.dma_start(out=dst, in_=view), writes=[r_wb[k], r_wbB[k]])
            return dst, [r_wb[k], r_wbB[k]]

        def wload2(viewA, viewB, inner=KC):
            k = cnt["wb"] % NWB
            cnt["wb"] += 1
            wt = wb[k][:, 0:4096].rearrange("p (a c n) -> p a c n", a=2, c=inner)
            S.dma("pool", lambda e: e.dma_start(out=wt[:, 0], in_=viewA), writes=[r_wb[k]])
            S.dma("pool", lambda e: e.dma_start(out=wt[:, 1], in_=viewB), writes=[r_wbB[k]])
            return wt, [r_wb[k], r_wbB[k]]

        def new_tmp():
            k = cnt["tmp"] % 2
            cnt["tmp"] += 1
            return tmpf[k], r_tmpf[k]

        def new_rstd():
            k = cnt["rstd"] % 2
            cnt["rstd"] += 1
            return rstd[k], r_rstd[k]

        def new_ps(pool=(0, 1, 2, 3)):
            key = "ps" if len(pool) == 4 else "ps45"
            k = pool[cnt[key] % len(pool)]
            cnt[key] += 1
            return PS[k], rPS[k]

        def mm(out, lhsT, rhs, start, stop, reads, writes):
            S.op("pe", lambda e: e.matmul(out, lhsT=lhsT, rhs=rhs, start=start, stop=stop),
                 reads=reads, writes=writes)

        def tp(out, in_, idn, reads, writes):
            S.op("pe", lambda e: e.transpose(out, in_, idn), reads=reads, writes=writes)

        def act(out, in_, func, reads, writes, bias=None, scale=None):
            kw = {}
            if bias is not None:
                kw["bias"] = bias
            if scale is not None:
                kw["scale"] = scale
            S.op("act", lambda e: e.activation(out=out, in_=in_, func=func, **kw), reads=reads, writes=writes)

        def tt(out, in0, in1, op, reads, writes, eng="dve"):
            S.op(eng, lambda e: e.tensor_tensor(out=out, in0=in0, in1=in1, op=op), reads=reads, writes=writes)

        def ts(out, in0, s1, s2, op0, op1, reads, writes, eng="dve"):
            if op1 is None:
                S.op(eng, lambda e: e.tensor_scalar(out=out, in0=in0, scalar1=s1, scalar2=None, op0=op0),
                     reads=reads, writes=writes)
            else:
                S.op(eng, lambda e: e.tensor_scalar(out=out, in0=in0, scalar1=s1, scalar2=s2, op0=op0, op1=op1),
                     reads=reads, writes=writes)

        def stt(out, in0, sc, in1, op0, op1, reads, writes):
            S.op("dve", lambda e: e.scalar_tensor_tensor(out=out, in0=in0, scalar=sc, in1=in1, op0=op0, op1=op1),
                 reads=reads, writes=writes)

        def recip(out, in_, reads, writes):
            S.op("dve", lambda e: e.reciprocal(out=out, in_=in_), reads=reads, writes=writes)

        def sp_dma(out, in_, reads=(), writes=()):
            S.dma("sp", lambda e: e.dma_start(out=out, in_=in_), reads=reads, writes=writes)

        def sum_bcast(src3, nchunks, n, reads, ps, rps):
            for c in range(nchunks):
                mm(ps[:, 0:n], ones[:, :], src3[:, c, 0:n], c == 0, c == nchunks - 1, list(reads) + [r_ones], [rps])

        def rsqrt_from_ps(ps, rps, n, inv_n):
            rt, rr = new_rstd()
            act(rt[:, 0:n], ps[:, 0:n], AF.Sqrt, [rps, r_vecs], [rr], bias=V("eps"), scale=inv_n)
            recip(rt[:, 0:n], rt[:, 0:n], [rr], [rr])
            return rt, rr

        sp_dma(vecs[:], vecs_d, writes=[r_vecs])
        sp_dma(tri[:], tri_d, writes=[r_tri])
        S.op("dve", lambda e: e.tensor_copy(out=trib[:], in_=tri[:]), reads=[r_tri], writes=[r_tri])
        sp_dma(identf[:], ident_d, writes=[r_ident])
        S.op("dve", lambda e: e.tensor_copy(out=ident[:], in_=identf[:]), reads=[r_ident], writes=[r_ident])
        S.op("dve", lambda e: e.memset(ones[:], 1.0), writes=[r_ones])
        sp_dma(csb[:], cT.rearrange("(c p) s -> p c s", p=128), writes=[r_csb])
        act(csbf[:], csb[:], AF.Silu, [r_csb], [r_csbf])
        xv = xT.rearrange("(c p) t -> p c t", p=128)
        for i, (a, b) in enumerate(TILES):
            bp = min(b, NPR)
            sp_dma(xres[:, :, a:bp], xv[:, :, FAR + a:FAR + bp], writes=[r_xres[i]])
        sp_dma(xres[:, :, NPR:T], xsT.rearrange("(c p) s -> p c s", p=128), writes=[r_xres[2]])
        sp_dma(idx_i[:], ptrep_d, writes=[r_idx])
        S.op("dve", lambda e: e.tensor_copy(out=idx_f[:], in_=idx_i[:]), reads=[r_idx], writes=[r_idx])
        ts(idx_f[:], idx_f[:], 32.0, V("pm16"), ALU.mult, ALU.add, [r_idx, r_vecs], [r_idx])
        S.op("dve", lambda e: e.tensor_copy(out=idx_i[:], in_=idx_f[:]), reads=[r_idx], writes=[r_idx])

        def modulation(l):
            wv = w_mod[l].rearrange("(c p) n -> p c n", p=128)
            for t in range(48):
                wt, rw = wload(wv[:, :, t * 256:(t + 1) * 256], [128, KC, 256])
                for m in range(2):
                    oc = 2 * t + m
                    ps, rps = new_ps()
                    for kc in range(KC):
                        mm(ps[:, 0:17], wt[:, kc, m * 128:(m + 1) * 128], csbf[:, kc, :], kc == 0, kc == KC - 1,
                           rw + [r_csbf], [rps])
                    act(modv[:, oc // 16, oc % 16, :], ps[:, 0:17], AF.Identity, [rps, r_vecs], [r_modv],
                        bias=V("b_mod", l * 96 + oc))
            for v, gname in ((1, "g_mix"), (4, "g_ffn")):
                g = bcast(V(gname, l * 16, 16), 2, 17)
                stt(modv[:, v], modv[:, v], 1.0, g, ALU.add, ALU.mult, [r_modv, r_vecs], [r_modv])

        def norm_h(vA, vB, sq3, r_sq, dst3, r_dst):
            for i, (a, b) in enumerate(TILES):
                n = b - a
                bp = min(b, NPR)
                npc = bp - a
                act(sq3[:, :, 0:n], xres[:, :, a:b], AF.Square, [r_xres[i]], [r_sq])
                ps, rps = new_ps((4, 5))
                sum_bcast(sq3, KC, n, [r_sq], ps, rps)
                rt, rr = rsqrt_from_ps(ps, rps, n, 1.0 / D)
                for kc in range(KC):
                    t_, rt_ = new_tmp()
                    stt(t_[:, 0:npc], xres[:, kc, a:bp], modv[:, vA, kc, 0:1], rt[:, 0:npc], ALU.mult, ALU.mult,
                        [r_xres[i], r_modv, rr], [rt_])
                    act(dst3[:, kc, a:bp], t_[:, 0:npc], AF.Identity, [rt_, r_modv], [r_dst[i]],
                        bias=modv[:, vB, kc, 0:1])
                if b > NPR:
                    sm = small[:, 0:KC * NS].rearrange("p (c s) -> p c s", c=KC)
                    tt(sm, xres[:, :, NPR:T], bcast(rt[:, npc:npc + NS], 1, KC), ALU.mult, [r_xres[i], rr], [r_small])
                    tt(sm, sm, modv[:, vA, :, 1:17], ALU.mult, [r_small, r_modv], [r_small])
                    tt(dst3[:, :, NPR:T], sm, modv[:, vB, :, 1:17], ALU.add, [r_small, r_modv], [r_dst[i]])

        def x_update(oc, i, ps, rps, vG, bias_col=None):
            a, b = TILES[i]
            bp = min(b, NPR)
            npc = bp - a
            if bias_col is None:
                stt(xres[:, oc, a:bp], ps[:, 0:npc], modv[:, vG, oc, 0:1], xres[:, oc, a:bp], ALU.mult, ALU.add,
                    [rps, r_modv, r_xres[i]], [r_xres[i]])
            else:
                t_, rt_ = new_tmp()
                ts(t_[:, 0:npc], ps[:, 0:npc], bias_col, modv[:, vG, oc, 0:1], ALU.add, ALU.mult,
                   [rps, r_vecs, r_modv], [rt_])
                tt(xres[:, oc, a:bp], xres[:, oc, a:bp], t_[:, 0:npc], ALU.add, [rt_, r_xres[i]], [r_xres[i]])
            if b > NPR:
                sm = small[:, 256:256 + NS]
                if bias_col is None:
                    tt(sm, ps[:, npc:npc + NS], modv[:, vG, oc, 1:17], ALU.mult, [rps, r_modv], [r_small])
                else:
                    stt(sm, ps[:, npc:npc + NS], bias_col, modv[:, vG, oc, 1:17], ALU.add, ALU.mult,
                        [rps, r_modv, r_vecs], [r_small])
                tt(xres[:, oc, NPR:T], xres[:, oc, NPR:T], sm, ALU.add, [r_small, r_xres[i]], [r_xres[i]])

        modulation(0)
        ckvT = arA[:, 0:4 * CKW].rearrange("p (c t) -> p c t", c=4)
        kpeT = arA[0:64, 4 * CKW:5 * CKW]
        sqA = arA[:, 5 * CKW:5 * CKW + KC * 358].rearrange("p (c t) -> p c t", c=KC)
        r_ckvT, r_kpeT, r_sqA = Res(), Res(), Res()
        qa = arB[:, 0:4 * T].rearrange("p (c t) -> p c t", c=4)
        r_qa = Res()
        kvf = arC[:, 0:4 * 358].rearrange("p (c t) -> p c t", c=4)
        sq4 = arC[:, 1432:1432 + 716].bitcast(BF16).rearrange("p (c t) -> p c t", c=4)
        kstg = arC[0:64, 2148:2148 + 358]
        ropes = arC[0:64, 2506:2506 + 716].rearrange("p (a t) -> p a t", a=2)
        r_kvf, r_sq4, r_kstg, r_ropes = Res(), Res(), Res(), Res()

        def proj4(wts, src3, rsrc, n, gname, out_bf, r_out, out_f32_dma=None):
            for c in range(4):
                wt, rw, co = wts[c]
                ps, rps = new_ps()
                for kc in range(KC):
                    mm(ps[:, 0:n], wt[:, kc, co:co + 128], src3[:, kc, 0:n], kc == 0, kc == KC - 1, rw + [rsrc], [rps])
                act(kvf[:, c, 0:n], ps[:, 0:n], AF.Copy, [rps], [r_kvf])
            act(sq4[:, :, 0:n], kvf[:, :, 0:n], AF.Square, [r_kvf], [r_sq4])
            ps, rps = new_ps((4, 5))
            sum_bcast(sq4, 4, n, [r_sq4], ps, rps)
            rt, rr = rsqrt_from_ps(ps, rps, n, 1.0 / 512)
            for c in range(4):
                stt(kvf[:, c, 0:n], kvf[:, c, 0:n], V(gname, c), rt[:, 0:n], ALU.mult, ALU.mult,
                    [r_kvf, rr, r_vecs], [r_kvf])
            act(out_bf, kvf[:, :, 0:n], AF.Copy, [r_kvf], [r_out])

        wkv_v = wkv_a.rearrange("(c p) n -> p c n", p=128)
        wqa_v = wq_a.rearrange("(c p) n -> p c n", p=128)

        def kv_path(src3, rsrc, n, rel0):
            w0, rw0 = wload(wkv_v[:, :, 0:256], [128, KC, 256])
            w1, rw1 = wload(wkv_v[:, :, 256:512], [128, KC, 256])
            wts = [(w0, rw0, 0), (w0, rw0, 128), (w1, rw1, 0), (w1, rw1, 128)]
            proj4(wts, src3, rsrc, n, "g_kv", ckvT[:, :, rel0:rel0 + n], r_ckvT)
            lo, hi = max(rel0, 1024), min(rel0 + n, NKEY)
            if hi > lo:
                sp_dma(ckv_o.rearrange("(c p) t -> p c t", p=128)[:, :, lo - 1024:hi - 1024],
                       kvf[:, :, lo - rel0:hi - rel0], reads=[r_kvf])
            if rel0 + n > NKEY:
                so = NKEY - rel0
                sp_dma(ckv_so.rearrange("(c p) t -> p c t", p=128), kvf[:, :, so:so + NS], reads=[r_kvf])
            wp, rwp = wload(wkv_v[:, :, 512:640], [128, KC, 128])
            sp_dma(ropes[:, :, 0:n], rope_d[:, :, rel0:rel0 + n], writes=[r_ropes])
            psa, rpa = new_ps()
            psb, rpb = new_ps()
            for kc in range(KC):
                mm(psa[0:64, 0:n], wp[:, kc, 0:64], src3[:, kc, 0:n], kc == 0, kc == KC - 1, rwp + [rsrc], [rpa])
            for kc in range(KC):
                mm(psb[0:64, 0:n], wp[:, kc, 64:128], src3[:, kc, 0:n], kc == 0, kc == KC - 1, rwp + [rsrc], [rpb])
            t1, rt1 = new_tmp()
            t2, rt2 = new_tmp()
            tt(t1[0:64, 0:n], psa[0:64, 0:n], ropes[:, 0, 0:n], ALU.mult, [rpa, r_ropes], [rt1])
            tt(t2[0:64, 0:n], psb[0:64, 0:n], ropes[:, 1, 0:n], ALU.mult, [rpb, r_ropes], [rt2])
            tt(kstg[:, 0:n], t1[0:64, 0:n], t2[0:64, 0:n], ALU.add, [rt1, rt2], [r_kstg])
            act(kpeT[:, rel0:rel0 + n], kstg[:, 0:n], AF.Copy, [r_kstg], [r_kpeT])
            if hi > lo:
                sp_dma(kpe_o[:, lo - 1024:hi - 1024], kstg[:, lo - rel0:hi - rel0], reads=[r_kstg])
            if rel0 + n > NKEY:
                so = NKEY - rel0
                sp_dma(kpe_so, kstg[:, so:so + NS], reads=[r_kstg])

        xctx = hbuf[:, 0:2 * KC * 330].bitcast(F32).rearrange("p (c t) -> p c t", c=KC)
        hctx = hbuf[:, 2 * KC * 330:3 * KC * 330].rearrange("p (c t) -> p c t", c=KC)
        r_xctx, r_hctx = Res(), Res()
        for (a, b) in CTXT:
            n = b - a
            sp_dma(xctx[:, :, 0:n], xv[:, :, a:b], writes=[r_xctx])
            act(sqA[:, :, 0:n], xctx[:, :, 0:n], AF.Square, [r_xctx], [r_sqA])
            ps, rps = new_ps((4, 5))
            sum_bcast(sqA, KC, n, [r_sqA], ps, rps)
            rt, rr = rsqrt_from_ps(ps, rps, n, 1.0 / D)
            for kc in range(KC):
                t_, rt_ = new_tmp()
                stt(t_[:, 0:n], xctx[:, kc, 0:n], modv[:, 1, kc, 0:1], rt[:, 0:n], ALU.mult, ALU.mult,
                    [r_xctx, r_modv, rr], [rt_])
                act(hctx[:, kc, 0:n], t_[:, 0:n], AF.Identity, [rt_, r_modv], [r_hctx], bias=modv[:, 0, kc, 0:1])
            kv_path(hctx, r_hctx, n, a)
        r_h = [alias([r_xctx, r_hctx]) for _ in range(3)]
        norm_h(1, 0, sqA, r_sqA, h3, r_h)
        for i, (a, b) in enumerate(TILES):
            kv_path(h3[:, :, a:b], r_h[i], b - a, FAR + a)
        for i, (a, b) in enumerate(TILES):
            w0, rw0 = wload(wqa_v[:, :, 0:256], [128, KC, 256])
            w1, rw1 = wload(wqa_v[:, :, 256:512], [128, KC, 256])
            wts = [(w0, rw0, 0), (w0, rw0, 128), (w1, rw1, 0), (w1, rw1, 128)]
            proj4(wts, h3[:, :, a:b], r_h[i], b - a, "g_q", qa[:, :, a:b], r_qa)

        r_arC2 = alias([r_kvf, r_sq4, r_kstg, r_ropes])
        ropeL = arC[0:64, 0:2 * T].rearrange("p (a t) -> p a t", a=2)
        sp_dma(ropeL, rope_d[:, :, FAR:FAR + T], writes=[r_arC2])

        wqb_v = wq_b.rearrange("(c p) n -> p c n", p=128)
        wuk_v = w_uk.rearrange("(c p) n -> p c n", p=128)
        wuv_v = w_uv.rearrange("(c p) n -> p c n", p=128)
        wukT_v = w_ukT.rearrange("p (h c) -> p h c", h=NH)
        wo_v = w_o.rearrange("(h p) n -> p h n", p=128)

        def q_head(wt, rw, hh, cols_a, cols_b, out_n, out_p, r_out):
            base = hh * 256
            c0 = cols_a
            while c0 < cols_b:
                c1 = min(c0 + 358, cols_b)
                n = c1 - c0
                o0 = c0 - cols_a
                ps, rps = new_ps()
                for kc in range(4):
                    mm(ps[:, 0:n], wt[:, kc, base:base + 128], qa[:, kc, c0:c1], kc == 0, kc == 3,
                       rw + [r_qa], [rps])
                act(out_n[:, o0:o0 + n], ps[:, 0:n], AF.Copy, [rps], [r_out])
                psa, rpa = new_ps()
                psb, rpb = new_ps()
                for kc in range(4):
                    mm(psa[0:64, 0:n], wt[:, kc, base + 128:base + 192], qa[:, kc, c0:c1], kc == 0, kc == 3,
                       rw + [r_qa], [rpa])
                for kc in range(4):
                    mm(psb[0:64, 0:n], wt[:, kc, base + 192:base + 256], qa[:, kc, c0:c1], kc == 0, kc == 3,
                       rw + [r_qa], [rpb])
                t1, rt1 = new_tmp()
                t2, rt2 = new_tmp()
                tt(t1[0:64, 0:n], psa[0:64, 0:n], ropeL[:, 0, c0:c1], ALU.mult, [rpa, r_arC2], [rt1])
                tt(t2[0:64, 0:n], psb[0:64, 0:n], ropeL[:, 1, c0:c1], ALU.mult, [rpb, r_arC2], [rt2])
                tt(out_p[:, o0:o0 + n], t1[0:64, 0:n], t2[0:64, 0:n], ALU.add, [rt1, rt2], [r_out])
                c0 = c1

        r_hfree = alias(r_h)
        o = 0
        GS = 4
        NCK = 4
        ck, kp = [], []
        for i in range(NCK):
            ck.append(hbuf[:, o:o + GS * 512].rearrange("p (s c) -> p s c", s=GS))
            o += GS * 512
        for i in range(NCK):
            kp.append(hbuf[:, o:o + GS * 64].rearrange("p (s c) -> p s c", s=GS))
            o += GS * 64
        ckTs, kpTs, ptss = [], [], []
        for i in range(2):
            ckTs.append(hbuf[:, o:o + 4 * GS * 128].rearrange("p (c k) -> p c k", c=4))
            o += 4 * GS * 128
        for i in range(2):
            kpTs.append(hbuf[0:64, o:o + GS * 128])
            o += GS * 128
        for i in range(2):
            ptss.append(hbuf[:, o:o + GS * 16].rearrange("p (s h) -> p s h", s=GS))
            o += GS * 16
        qlat = hbuf[:, o:o + 1024].rearrange("p (c h s) -> p c h s", c=4, h=NH)
        o += 1024
        olatT = hbuf[:, o:o + 1024].rearrange("p (c h s) -> p c h s", c=4, h=NH)
        o += 1024
        xb16 = arC[0:16, 2176:2176 + 640].bitcast(BF16)
        cnew = xb16[:, 0:512]
        olb = xb16[:, 512:1024]
        ptn = xb16[:, 1024:1040]
        assert o <= KC * T
        r_ck = [alias([r_hfree]) for _ in range(NCK)]
        r_kp = [alias([r_hfree]) for _ in range(NCK)]
        r_ckTs = [alias([r_hfree]) for _ in range(2)]
        r_kpTs = [alias([r_hfree]) for _ in range(2)]
        r_ptss = [alias([r_hfree]) for _ in range(2)]
        r_qlat, r_olatT = alias([r_hfree]), alias([r_hfree])
        r_cnew, r_ptn, r_olb = (alias([r_kvf, r_sq4, r_kstg, r_ropes]) for _ in range(3))

        for g in range(4):
            wt, rw = wload(wqb_v[:, :, g * 1024:(g + 1) * 1024], [128, 4, 1024])
            for hh in range(4):
                h_ = g * 4 + hh
                q_head(wt, rw, hh, NPR, T, qs_n[:, h_, :], qs_p[:, h_, :], r_qs)
        for g in range(4):
            wt, rw = wload(wukT_v[:, g * 4:(g + 1) * 4, :], [128, 4, 512])
            for hh in range(4):
                h_ = g * 4 + hh
                ps, rps = new_ps()
                for c in range(4):
                    mm(ps[:, c * 16:(c + 1) * 16], wt[:, hh, c * 128:(c + 1) * 128], qs_n[:, h_, :], True, True,
                       rw + [r_qs], [rps])
                act(qlat[:, :, h_, :], ps[:, 0:64].rearrange("p (c s) -> p c s", c=4), AF.Copy, [rps], [r_qlat])
        ps, rps = new_ps()
        psb16 = ps[:, :].bitcast(BF16)
        for c in range(4):
            tp(psb16[0:16, c * 128:(c + 1) * 128], ckvT[:, c, NKEY:NKEY + NS], ident[:, :], [r_ckvT, r_ident], [rps])
        act(cnew, psb16[0:16, 0:512], AF.Copy, [rps], [r_cnew])

        cck_rows = cckv
        ckp_rows = ckpe
        psO, rpO = PS[6], rPS[6]
        psl, rpl = PS[7], rPS[7]
        NGS = 64 // GS

        def s_gather(g):
            cb_ = g % NCK
            S.dma("pool", lambda e: e.indirect_dma_start(
                out=ck[cb_].rearrange("p s c -> p (s c)"), out_offset=None, in_=cckv[:, :],
                in_offset=bass.IndirectOffsetOnAxis(ap=idx_i[:, g:g + 1], axis=0)),
                reads=[r_idx], writes=[r_ck[cb_]])
            S.dma("pool", lambda e: e.indirect_dma_start(
                out=kp[cb_].rearrange("p s c -> p (s c)"), out_offset=None, in_=ckpe[:, :],
                in_offset=bass.IndirectOffsetOnAxis(ap=idx_i[:, g:g + 1], axis=0)),
                reads=[r_idx], writes=[r_kp[cb_]])

        def s_transpose(g):
            cb_, tb = g % NCK, g % 2
            cT_, rcT_ = ckTs[tb], r_ckTs[tb]
            kT_, rkT_ = kpTs[tb], r_kpTs[tb]
            for s2 in range(GS // 2):
                ps, rps = new_ps()
                pb = ps[:, :].bitcast(BF16)
                for s in (2 * s2, 2 * s2 + 1):
                    for c in range(4):
                        o_ = (s - 2 * s2) * 512 + c * 128
                        tp(pb[:, o_:o_ + 128], ck[cb_][:, s, c * 128:(c + 1) * 128], ident[:, :],
                           [r_ck[cb_], r_ident], [rps])
                src = pb[:, 0:1024].rearrange("p (s c k) -> p c s k", s=2, c=4)
                dst = cT_[:, :, 2 * s2 * 128:(2 * s2 + 2) * 128].rearrange("p c (s k) -> p c s k", s=2)
                if s2 % 2 == 0:
                    act(dst, src, AF.Copy, [rps], [rcT_])
                else:
                    S.op("dve", lambda e, dst=dst, src=src: e.tensor_copy(out=dst, in_=src),
                         reads=[rps], writes=[rcT_])
            ps, rps = new_ps()
            pb = ps[:, :].bitcast(BF16)
            for s in range(GS):
                tp(pb[0:64, s * 128:(s + 1) * 128], kp[cb_][:, s, :], ident[:, :], [r_kp[cb_], r_ident], [rps])
            S.op("dve", lambda e, pb=pb, kT_=kT_: e.tensor_copy(out=kT_[:, :], in_=pb[0:64, 0:GS * 128]),
                 reads=[rps], writes=[rkT_])

        def s_scores(g):
            b_, tb = g // NGS, g % 2
            cT_, rcT_ = ckTs[tb], r_ckTs[tb]
            kT_, rkT_ = kpTs[tb], r_kpTs[tb]
            psS, rpS = new_ps((4, 5))
            for s in range(GS):
                for c in range(4):
                    mm(psS[:, s * 16:(s + 1) * 16], cT_[:, c, s * 128:(s + 1) * 128], qlat[:, c, :, b_],
                       c == 0, False, [rcT_, r_qlat], [rpS])
                mm(psS[:, s * 16:(s + 1) * 16], kT_[:, s * 128:(s + 1) * 128], qs_p[:, :, b_], False, True,
                   [rkT_, r_qs], [rpS])
            act(ptss[tb], psS[:, 0:GS * 16].rearrange("p (s h) -> p s h", s=GS), AF.Exp, [rpS], [r_ptss[tb]],
                scale=SCALE)

        def s_pv(g):
            b_, j, cb_, tb = g // NGS, g % NGS, g % NCK, g % 2
            for s in range(GS):
                mm(psO[0:16, 0:512], ptss[tb][:, s, :], ck[cb_][:, s, :], j == 0 and s == 0, False,
                   [r_ptss[tb], r_ck[cb_]], [rpO])
            for s in range(GS):
                mm(psl[0:16, 0:1], ptss[tb][:, s, :], ones[:, 0:1], j == 0 and s == 0, False,
                   [r_ptss[tb], r_ones], [rpl])
            if j < NGS - 1:
                return
            psN, rpN = new_ps((4, 5))
            for c in range(4):
                mm(psN[0:16, 0:16], ckvT[:, c, NKEY:NKEY + NS], qlat[:, c, :, b_], c == 0, False,
                   [r_ckvT, r_qlat], [rpN])
            mm(psN[0:16, 0:16], kpeT[:, NKEY:NKEY + NS], qs_p[:, :, b_], False, True, [r_kpeT, r_qs], [rpN])
            t_, rt_ = new_tmp()
            act(t_[0:16, 0:16], psN[0:16, 0:16], AF.Exp, [rpN], [rt_], scale=SCALE)
            ts(ptn, t_[0:16, 0:16], identf[0:16, b_:b_ + 1], None, ALU.mult, None, [rt_, r_ident], [r_ptn])
            mm(psO[0:16, 0:512], ptn, cnew, False, True, [r_ptn, r_cnew], [rpO])
            mm(psl[0:16, 0:1], ptn, ones[0:16, 0:1], False, True, [r_ptn, r_ones], [rpl])
            rl = small[0:16, 300:301]
            recip(rl, psl[0:16, 0:1], [rpl], [r_small])
            ts(olb, psO[0:16, 0:512], rl, None, ALU.mult, None, [rpO, r_small], [r_olb])
            ps, rps = new_ps()
            pb = ps[:, :].bitcast(BF16)
            for c in range(4):
                tp(pb[:, c * 16:(c + 1) * 16], olb[:, c * 128:(c + 1) * 128], ident[0:16, 0:16], [r_olb, r_ident], [rps])
            act(olatT[:, :, :, b_], pb[:, 0:64].rearrange("p (c h) -> p c h", c=4), AF.Copy, [rps], [r_olatT])

        NG = NS * NGS
        PF = NCK - 1
        for g in range(min(PF, NG)):
            s_gather(g)
        s_transpose(0)
        for g in range(NG):
            if g + PF < NG:
                s_gather(g + PF)
            s_scores(g)
            if g + 1 < NG:
                s_transpose(g + 1)
            s_pv(g)
        for g in range(4):
            wt, rw = wload(wuv_v[:, :, g * 512:(g + 1) * 512], [128, 4, 512])
            for hh in range(4):
                h_ = g * 4 + hh
                ps, rps = new_ps()
                for c in range(4):
                    mm(ps[:, 0:NS], wt[:, c, hh * 128:(hh + 1) * 128], olatT[:, c, h_, :], c == 0, c == 3,
                       rw + [r_olatT], [rps])
                act(osT[:, h_, :], ps[:, 0:NS], AF.Copy, [rps], [r_osT])

        r_hfree2 = alias(r_ck + r_kp + r_ckTs + r_kpTs + r_ptss + [r_qlat, r_olatT, r_cnew, r_ptn, r_olb])
        HS = 6244
        knT, vh, qn, qp, r_hd = [], [], [], [], []
        for i in range(2):
            o = i * HS
            knT.append(hbuf[:, o:o + 2048])
            vh.append(hbuf[:, o + 2048:o + 4096].rearrange("p (k d) -> p k d", k=16))
            qn.append(hbuf[:, o + 4096:o + 4096 + T])
            qp.append(hbuf[0:64, o + 4096 + T:o + 4096 + 2 * T])
            r_hd.append({k: alias([r_hfree2]) for k in ("kn", "v", "q")})
        ptb = [hbuf[:, 2 * HS + i * 512:2 * HS + (i + 1) * 512] for i in range(4)]
        r_ptb = [alias([r_hfree2]) for _ in range(4)]
        oT = arA[:, 5 * CKW:5 * CKW + 4 * T].rearrange("p (h t) -> p h t", h=4)
        r_oT = alias([r_sqA])
        npt = [0]
        for g in range(4):
            wtq, rwq = wload(wqb_v[:, :, g * 1024:(g + 1) * 1024], [128, 4, 1024])
            wkv2, rwk = wload2(wuk_v[:, :, g * 512:(g + 1) * 512], wuv_v[:, :, g * 512:(g + 1) * 512], inner=4)
            wtk, wtv, rwv = wkv2[:, 0], wkv2[:, 1], rwk
            for hh in range(4):
                h_ = g * 4 + hh
                hb = h_ % 2
                R = r_hd[hb]
                q_head(wtq, rwq, hh, 0, NPR, qn[hb][:, 0:NPR], qp[hb][:, 0:NPR], R["q"])
                for kt in range(4):
                    ps, rps = new_ps()
                    for c in range(4):
                        mm(ps[:, 0:512], wtk[:, c, hh * 128:(hh + 1) * 128], ckvT[:, c, kt * 512:(kt + 1) * 512],
                           c == 0, c == 3, rwk + [r_ckvT], [rps])
                    if kt % 2 == 0:
                        act(knT[hb][:, kt * 512:(kt + 1) * 512], ps[:, 0:512], AF.Copy, [rps], [R["kn"]])
                    else:
                        S.op("dve", lambda e, hb=hb, kt=kt, ps=ps: e.tensor_copy(
                            out=knT[hb][:, kt * 512:(kt + 1) * 512], in_=ps[:, 0:512]), reads=[rps], writes=[R["kn"]])
                for k4 in range(4):
                    ps, rps = new_ps()
                    for kk in range(4):
                        kb = k4 * 4 + kk
                        for c in range(4):
                            mm(ps[:, kk * 128:(kk + 1) * 128], ckvT[:, c, kb * 128:(kb + 1) * 128],
                               wtv[:, c, hh * 128:(hh + 1) * 128], c == 0, c == 3, rwv + [r_ckvT], [rps])
                    dst = vh[hb][:, k4 * 4:(k4 + 1) * 4, :]
                    src = ps[:, 0:512].rearrange("p (k d) -> p k d", k=4)
                    if k4 % 2 == 0:
                        S.op("dve", lambda e, dst=dst, src=src: e.tensor_copy(out=dst, in_=src),
                             reads=[rps], writes=[R["v"]])
                    else:
                        act(dst, src, AF.Copy, [rps], [R["v"]])
                pend = []

                def flush(keep):
                    while len(pend) > keep:
                        pend.pop(0)()

                for qi, (qa_, qb_) in enumerate(QT):
                    nq = qb_ - qa_
                    r0 = FAR + qa_
                    r1 = FAR + qb_
                    nkb = (r1 - 1) // 128 + 1
                    ob = (6, 7) if (qi + h_) % 2 == 0 else (4, 5)
                    psO, rpO = PS[ob[0]], rPS[ob[0]]
                    psL, rpL = PS[ob[1]], rPS[ob[1]]
                    for kb in range(nkb):
                        k0 = kb * 128
                        c0 = 0 if qa_ == 0 else max(0, k0 - r0)
                        nn = nq - c0
                        psS, rpS = new_ps()
                        mm(psS[:, 0:nn], knT[hb][:, k0:k0 + 128], qn[hb][:, qa_ + c0:qb_], True, False,
                           [R["kn"], R["q"]], [rpS])
                        mm(psS[:, 0:nn], kpeT[:, k0:k0 + 128], qp[hb][:, qa_ + c0:qb_], False, True,
                           [r_kpeT, R["q"]], [rpS])
                        pi = npt[0] % 4
                        npt[0] += 1
                        bias = V("cb") if kb < 8 else V("zero")
                        act(ptb[pi][:, 0:nn], psS[:, 0:nn], AF.Exp, [rpS, r_vecs], [r_ptb[pi]], bias=bias, scale=SCALE)
                        if qa_ == 0 and kb == 7:
                            tt(ptb[pi][:, 0:nn], ptb[pi][:, 0:nn], trib[:, 128:128 + HALO], ALU.mult,
                               [r_ptb[pi], r_tri], [r_ptb[pi]])
                        elif qa_ > 0 and k0 >= r0:
                            tt(ptb[pi][:, 0:128], ptb[pi][:, 0:128], trib[:, 0:128], ALU.mult,
                               [r_ptb[pi], r_tri], [r_ptb[pi]])

                        def pv(kb=kb, c0=c0, nn=nn, nq=nq, pi=pi, nkb=nkb, psO=psO, rpO=rpO, psL=psL, rpL=rpL,
                               qa_=qa_):
                            mm(psO[:, c0:nq], vh[hb][:, kb, :], ptb[pi][:, 0:nn], kb == 0, kb == nkb - 1,
                               [R["v"], r_ptb[pi]], [rpO])
                            mm(psL[:, c0:nq], ones[:, :], ptb[pi][:, 0:nn], kb == 0, kb == nkb - 1,
                               [r_ones, r_ptb[pi]], [rpL])
                            if kb < nkb - 1:
                                return
                            for (ha, hb2) in ((0, min(nq, 256)), (256, nq)):
                                if hb2 <= ha:
                                    continue
                                w_ = hb2 - ha
                                rt, rr = new_rstd()
                                ts(rt[:, 0:w_], psL[:, ha:hb2], V("tiny"), None, ALU.add, None, [rpL, r_vecs], [rr])
                                recip(rt[:, 0:w_], rt[:, 0:w_], [rr], [rr])
                                tt(oT[:, hh, qa_ + ha:qa_ + hb2], psO[:, ha:hb2], rt[:, 0:w_], ALU.mult,
                                   [rpO, rr], [r_oT])
                        pend.append(pv)
                        flush(2)
                flush(0)
                S.op("dve", lambda e, hh=hh, h_=h_: e.tensor_copy(out=oT[:, hh, NPR:T], in_=osT[:, h_, :]),
                     reads=[r_osT], writes=[r_oT])
            for half in range(2):
                wt, rw = wload(wo_v[:, g * 4:(g + 1) * 4, half * 1024:(half + 1) * 1024], [128, 4, 1024])
                for m in range(8):
                    oc = half * 8 + m
                    for i, (a, b) in enumerate(TILES):
                        n = b - a
                        ps, rps = new_ps()
                        for hh in range(4):
                            mm(ps[:, 0:n], wt[:, hh, m * 128:(m + 1) * 128], oT[:, hh, a:b], hh == 0, hh == 3,
                               rw + [r_oT], [rps])
                        x_update(oc, i, ps, rps, 2)

        def ffn(l, prev_h, prev_A):
            act1 = arA[:, 0:8 * T].rearrange("p (j t) -> p j t", j=8)
            r_act = alias(prev_A)
            sqF = arA[:, 8 * T:8 * T + KC * 358].rearrange("p (c t) -> p c t", c=KC)
            r_sqF = alias(prev_A)
            r_hh = [alias(prev_h) for _ in range(3)]
            norm_h(4, 3, sqF, r_sqF, h3, r_hh)
            wg_v = w_gate[l].rearrange("(c p) n -> p c n", p=128)
            wu_v = w_up[l].rearrange("(c p) n -> p c n", p=128)
            wd_v = w_down[l].rearrange("(j p) n -> p j n", p=128)
            sf_v = sffn[l].rearrange("(j p) s k -> p j s k", p=128)
            sfo_v = ffn_so[l].rearrange("(j p) s k -> p j s k", p=128)
            for (j0, j1) in FGROUPS:
                nj = j1 - j0
                sp_dma(sfst[:, 0:nj], sf_v[:, j0:j1], writes=[r_sfst])
                for j in range(j0, j1):
                    jj = j - j0
                    wt, rw = wload2(wg_v[:, :, j * 128:(j + 1) * 128], wu_v[:, :, j * 128:(j + 1) * 128])
                    wc = [V("w_fc", (l * NJ + j) * 3 + t_) for t_ in range(3)]
                    bc_ = V("b_fc", l * NJ + j)
                    for i, (a, b) in enumerate(TILES):
                        n = b - a
                        bp = min(b, NPR)
                        npc = bp - a
                        psG, rpG = new_ps()
                        psU, rpU = new_ps()
                        for kc in range(KC):
                            mm(psG[:, 0:n], wt[:, 0, kc, :], h3[:, kc, a:b], kc == 0, kc == KC - 1, rw + [r_hh[i]], [rpG])
                        for kc in range(KC):
                            mm(psU[:, 0:n], wt[:, 1, kc, :], h3[:, kc, a:b], kc == 0, kc == KC - 1, rw + [r_hh[i]], [rpU])
                        gi = i % 2
                        gb, rg = gbuf[gi], r_gbuf[gi]
                        act(gb[:, 2:2 + n], psG[:, 0:n], AF.Copy, [rpG], [rg])
                        if i == 0:
                            S.op("dve", lambda e, gb=gb: e.memset(gb[:, 0:2], 0.0), writes=[rg])
                            ts(gb[:, 2:2 + HALO], gb[:, 2:2 + HALO], V("hm"), None, ALU.mult, None, [rg, r_vecs], [rg])
                        if i < 2:
                            gn, rgn = gbuf[(i + 1) % 2], r_gbuf[(i + 1) % 2]
                            act(gn[:, 0:2], gb[:, n:n + 2], AF.Copy, [rg], [rgn])
                        else:
                            S.op("dve", lambda e, gb=gb, j=j, npc=npc: e.tensor_copy(
                                out=pfo[:, j, :], in_=gb[:, npc:npc + 2]), reads=[rg], writes=[r_pfo])
                        t_, rt_ = new_tmp()
                        ts(t_[:, 0:npc], gb[:, 2:2 + npc], wc[2], bc_, ALU.mult, ALU.add, [rg, r_vecs], [rt_])
                        stt(t_[:, 0:npc], gb[:, 1:1 + npc], wc[1], t_[:, 0:npc], ALU.mult, ALU.add, [rg, rt_, r_vecs], [rt_])
                        stt(t_[:, 0:npc], gb[:, 0:npc], wc[0], t_[:, 0:npc], ALU.mult, ALU.add, [rg, rt_, r_vecs], [rt_])
                        act(t_[:, 0:npc], t_[:, 0:npc], AF.Silu, [rt_], [rt_])
                        tt(act1[:, jj, a:bp], t_[:, 0:npc], psU[:, 0:npc], ALU.mult, [rt_, rpU], [r_act])
                        if b > NPR:
                            sm = small[:, 320:320 + NS]
                            gs = gb[:, 2 + npc:2 + npc + NS]
                            ts(sm, sfst[:, jj, :, 0], wc[0], bc_, ALU.mult, ALU.add, [r_sfst, r_vecs], [r_small])
                            stt(sm, sfst[:, jj, :, 1], wc[1], sm, ALU.mult, ALU.add, [r_sfst, r_small, r_vecs], [r_small])
                            stt(sm, gs, wc[2], sm, ALU.mult, ALU.add, [rg, r_small, r_vecs], [r_small])
                            act(sm, sm, AF.Silu, [r_small], [r_small])
                            tt(act1[:, jj, NPR:T], sm, psU[:, npc:npc + NS], ALU.mult, [r_small, rpU], [r_act])
                            S.op("dve", lambda e, jj=jj: e.tensor_copy(out=sfst[:, jj, :, 0], in_=sfst[:, jj, :, 1]),
                                 reads=[r_sfst], writes=[r_sfst])
                            S.op("dve", lambda e, jj=jj, gs=gs: e.tensor_copy(out=sfst[:, jj, :, 1], in_=gs),
                                 reads=[rg, r_sfst], writes=[r_sfst])
                sp_dma(sfo_v[:, j0:j1], sfst[:, 0:nj], reads=[r_sfst])
                for cg in range(4):
                    wt, rw = wload(wd_v[:, j0:j1, cg * 512:(cg + 1) * 512], [128, nj, 512])
                    for m in range(4):
                        oc = cg * 4 + m
                        for i, (a, b) in enumerate(TILES):
                            n = b - a
                            ps, rps = new_ps()
                            for jj in range(nj):
                                mm(ps[:, 0:n], wt[:, jj, m * 128:(m + 1) * 128], act1[:, jj, a:b], jj == 0, jj == nj - 1,
                                   rw + [r_act], [rps])
                            x_update(oc, i, ps, rps, 5)
            sp_dma(ffn_o[l].rearrange("(j p) k -> p j k", p=128), pfo[:, :, :], reads=[r_pfo])
            return r_hh, [r_act, r_sqF]

        prev_h = [r_hfree2] + [r_hd[i][k] for i in range(2) for k in ("kn", "v", "q")] + r_ptb
        prev_h, prev_A = ffn(0, prev_h, [r_ckvT, r_kpeT, r_sqA, r_oT])

        modulation(1)
        r_hc = [alias(prev_h) for _ in range(3)]
        sqC = arC[:, 0:KC * 358 // 2].bitcast(BF16).rearrange("p (c t) -> p c t", c=KC)
        r_sqC = alias([r_arC2])
        norm_h(1, 0, sqC, r_sqC, h3, r_hc)
        ybf = arA[:, :].rearrange("p (c t) -> p c t", c=KC)
        r_y = [alias(prev_A) for _ in range(3)]
        UW = 30 + NPR
        ub = arC[:, 0:UW // 2].bitcast(BF16)
        cs = arC[:, 544:544 + 480].rearrange("p (s k) -> p s k", s=NS)
        cso = arC[:, 1024:1024 + 480].rearrange("p (s k) -> p s k", s=NS)
        r_ub, r_cs, r_cso = (alias([r_sqC]) for _ in range(3))
        dg = arB[:, 0:31 * 128].rearrange("p (k m) -> p k m", k=31)
        r_dg = alias([r_qa])
        cpo = sb("cpo", [128, KC, 30], F32)
        r_cpo = Res()
        S.op("dve", lambda e: e.memset(ub[:, 0:30], 0.0), writes=[r_ub])
        w1_v = w_pw1.rearrange("(c p) n -> p c n", p=128)
        sc_v = sconv.rearrange("(c p) s k -> p c s k", p=128)
        cso_v = conv_so.rearrange("(c p) s k -> p c s k", p=128)
        for c in range(KC):
            wt, rw = wload2(w1_v[:, :, c * 128:(c + 1) * 128], w1_v[:, :, D + c * 128:D + (c + 1) * 128])
            sp_dma(cs, sc_v[:, c], writes=[r_cs])
            tt(dg, bcast(ident[:, :], 1, 31), bcast(V("w_dw", c * 31, 31), 2, 128), ALU.mult,
               [r_ident, r_vecs], [r_dg], eng="pool")
            us = small[:, 340:340 + NS]
            for i, (a, b) in enumerate(TILES):
                n = b - a
                bp = min(b, NPR)
                npc = bp - a
                psA, rpA = new_ps()
                psB, rpB = new_ps()
                for kc in range(KC):
                    mm(psA[:, 0:n], wt[:, 0, kc, :], h3[:, kc, a:b], kc == 0, kc == KC - 1, rw + [r_hc[i]], [rpA])
                for kc in range(KC):
                    mm(psB[:, 0:n], wt[:, 1, kc, :], h3[:, kc, a:b], kc == 0, kc == KC - 1, rw + [r_hc[i]], [rpB])
                t_, rt_ = new_tmp()
                act(t_[:, 0:n], psB[:, 0:n], AF.Sigmoid, [rpB, r_vecs], [rt_], bias=V("b_pw1", 16 + c))
                stt(ub[:, 30 + a:30 + bp], psA[:, 0:npc], V("b_pw1", c), t_[:, 0:npc], ALU.add, ALU.mult,
                    [rpA, rt_, r_vecs], [r_ub])
                if i == 0:
                    ts(ub[:, 30:30 + HALO], ub[:, 30:30 + HALO], V("hm"), None, ALU.mult, None, [r_ub, r_vecs], [r_ub])
                if b > NPR:
                    stt(us, psA[:, npc:npc + NS], V("b_pw1", c), t_[:, npc:npc + NS], ALU.add, ALU.mult,
                        [rpA, rt_, r_vecs], [r_small])
                    stt(cpo[:, c, :], psA[:, npc - 30:npc], V("b_pw1", c), t_[:, npc - 30:npc], ALU.add, ALU.mult,
                        [rpA, rt_, r_vecs], [r_cpo])
            for i, (a, b) in enumerate(TILES):
                bp = min(b, NPR)
                npc = bp - a
                psY, rpY = new_ps()
                for kx in range(31):
                    mm(psY[:, 0:npc], dg[:, kx, :], ub[:, a + kx:a + kx + npc], kx == 0, kx == 30,
                       [r_dg, r_ub], [rpY])
                act(ybf[:, c, a:bp], psY[:, 0:npc], AF.Identity, [rpY, r_vecs], [r_y[i]], bias=V("b_dw", c))
            tt(cso, cs, bcast(V("w_dw", c * 31, 30), 1, NS), ALU.mult, [r_cs, r_vecs], [r_cso])
            ys = small[:, 360:360 + NS]
            S.op("dve", lambda e, ys=ys: e.tensor_reduce(out=ys, in_=cso, axis=AX.X, op=ALU.add),
                 reads=[r_cso], writes=[r_small])
            stt(ys, us, V("w_dw", c * 31 + 30), ys, ALU.mult, ALU.add, [r_small, r_vecs], [r_small])
            ts(ybf[:, c, NPR:T], ys, V("b_dw", c), None, ALU.add, None, [r_small, r_vecs], [r_y[2]])
            S.op("dve", lambda e: e.tensor_copy(out=cso[:, :, 0:29], in_=cs[:, :, 1:30]), reads=[r_cs, r_cso], writes=[r_cso])
            S.op("dve", lambda e, us=us: e.tensor_copy(out=cso[:, :, 29], in_=us), reads=[r_small, r_cso], writes=[r_cso])
            sp_dma(cso_v[:, c], cso, reads=[r_cso])
        sp_dma(conv_o.rearrange("(c p) k -> p c k", p=128), cpo[:, :, :], reads=[r_cpo])
        r_z = [alias(r_hc) for _ in range(3)]
        sqL = arC[:, 0:KC * 358 // 2].bitcast(BF16).rearrange("p (c t) -> p c t", c=KC)
        r_sqL = alias([r_ub, r_cs, r_cso])
        mu, r_mu = gbuf[0], alias([r_gbuf[0]])
        rs_, r_rs = gbuf[1], alias([r_gbuf[1]])
        for i, (a, b) in enumerate(TILES):
            n = b - a
            act(sqL[:, :, 0:n], ybf[:, :, a:b], AF.Square, [r_y[i]], [r_sqL])
            psM, rpM = new_ps((4, 5))
            sum_bcast(ybf[:, :, a:b], KC, n, [r_y[i]], psM, rpM)
            psQ, rpQ = new_ps((4, 5))
            sum_bcast(sqL, KC, n, [r_sqL], psQ, rpQ)
            act(mu[:, 0:n], psM[:, 0:n], AF.Copy, [rpM], [r_mu], scale=1.0 / D)
            t_, rt_ = new_tmp()
            act(t_[:, 0:n], psM[:, 0:n], AF.Square, [rpM], [rt_], scale=1.0 / D)
            stt(rs_[:, 0:n], psQ[:, 0:n], 1.0 / D, t_[:, 0:n], ALU.mult, ALU.subtract, [rpQ, rt_], [r_rs])
            act(rs_[:, 0:n], rs_[:, 0:n], AF.Sqrt, [r_rs, r_vecs], [r_rs], bias=V("eps"))
            recip(rs_[:, 0:n], rs_[:, 0:n], [r_rs], [r_rs])
            for c in range(KC):
                t_, rt_ = new_tmp()
                tt(t_[:, 0:n], ybf[:, c, a:b], mu[:, 0:n], ALU.subtract, [r_y[i], r_mu], [rt_])
                tt(t_[:, 0:n], t_[:, 0:n], rs_[:, 0:n], ALU.mult, [rt_, r_rs], [rt_])
                act(h3[:, c, a:b], t_[:, 0:n], AF.Silu, [rt_, r_vecs], [r_z[i]], bias=V("ln_b", c), scale=V("ln_g", c))
        w2_v = w_pw2.rearrange("(c p) n -> p c n", p=128)
        for t in range(8):
            wt, rw = wload(w2_v[:, :, t * 256:(t + 1) * 256], [128, KC, 256])
            for m in range(2):
                oc = 2 * t + m
                for i, (a, b) in enumerate(TILES):
                    n = b - a
                    ps, rps = new_ps()
                    for kc in range(KC):
                        mm(ps[:, 0:n], wt[:, kc, m * 128:(m + 1) * 128], h3[:, kc, a:b], kc == 0, kc == KC - 1,
                           rw + [r_z[i]], [rps])
                    x_update(oc, i, ps, rps, 2, bias_col=V("b_pw2", oc))
        r_gbuf[0] = alias([r_mu])
        r_gbuf[1] = alias([r_rs])
        prev_h, prev_A = ffn(1, r_z, r_y)

        ystg = arA[:, 0:2 * KC * 358].bitcast(F32).rearrange("p (c t) -> p c t", c=KC)
        r_ystg = alias(prev_A)
        sqE = arA[:, 2 * KC * 358:3 * KC * 358].rearrange("p (c t) -> p c t", c=KC)
        r_sqE = alias(prev_A)
        yv = yT.rearrange("(c p) t -> p c t", p=128)
        for i, (a, b) in enumerate(TILES):
            n = b - a
            act(sqE[:, :, 0:n], xres[:, :, a:b], AF.Square, [r_xres[i]], [r_sqE])
            ps, rps = new_ps((4, 5))
            sum_bcast(sqE, KC, n, [r_sqE], ps, rps)
            rt, rr = rsqrt_from_ps(ps, rps, n, 1.0 / D)
            for kc in range(KC):
                stt(ystg[:, kc, 0:n], xres[:, kc, a:b], V("g_fin", kc), rt[:, 0:n], ALU.mult, ALU.mult,
                    [r_xres[i], rr, r_vecs], [r_ystg])
            lo, hi = max(a, HALO), min(b, NPR)
            sp_dma(yv[:, :, lo - HALO:hi - HALO], ystg[:, :, lo - a:hi - a], reads=[r_ystg])
            if b > NPR:
                sp_dma(ysT.rearrange("(c p) s -> p c s", p=128), ystg[:, :, NPR - a:T - a], reads=[r_ystg])
        S.emit()
    return nc


def _rope_tables(half):
    inv = (10000.0 ** (-np.arange(32, dtype=np.float32) / np.float32(32))).astype(np.float32)
    pos = np.empty(CKW, np.float32)
    pos[:NKEY] = np.arange(NKEY, dtype=np.float32) + np.float32(half * 1024 - 1024)
    pos[NKEY:] = np.float32(8192.0)
    ang = (pos[None, :] * inv[:, None]).astype(np.float32)
    cos, sin = np.cos(ang).astype(np.float32), np.sin(ang).astype(np.float32)
    tab = np.empty((64, 2, CKW), np.float32)
    tab[:32, 0] = cos
    tab[32:, 0] = cos
    tab[:32, 1] = -sin
    tab[32:, 1] = sin
    return tab


def _fm(v):
    return np.ascontiguousarray(np.asarray(v, np.float32).reshape(-1, 128).T)


def prepare(inp, cores):
    f = lambda k: np.asarray(inp[k], np.float32)
    x_prompt, x_sample = f("x_prompt"), f("x_sample")
    shared = {}
    shared["cache_ckv"] = f("cache_ckv").reshape(NPOOL * 32, 4 * 512)
    shared["cache_kpe"] = f("cache_kpe").reshape(NPOOL * 32, 4 * 64)
    shared["w_mod"] = f("w_mod")
    shared["wq_a"] = f("wq_a")[0]
    wqb = f("wq_b")[0].reshape(512, NH, 192)
    shared["wq_b"] = np.ascontiguousarray(np.concatenate(
        [wqb, wqb[:, :, 160:192], wqb[:, :, 128:160]], axis=2).reshape(512, NH * 256))
    wkv = f("wkv_a")[0]
    shared["wkv_a"] = np.ascontiguousarray(np.concatenate([wkv, wkv[:, 544:576], wkv[:, 512:544]], axis=1))
    wuk = f("w_uk")[0]
    shared["w_uk"] = wuk.reshape(512, NH * 128)
    shared["w_ukT"] = np.ascontiguousarray(wuk.transpose(2, 1, 0).reshape(128, NH * 512))
    shared["w_uv"] = f("w_uv")[0].reshape(512, NH * 128)
    shared["w_o"] = f("w_o")[0]
    shared["w_pw1"] = f("conv_w_pw1")[0]
    shared["w_pw2"] = f("conv_w_pw2")[0]
    shared["w_gate"] = f("ffn_w_gate")
    shared["w_up"] = f("ffn_w_up")
    shared["w_down"] = f("ffn_w_down")
    tri = np.zeros((128, 128 + HALO), np.float32)
    k = np.arange(128)[:, None]
    tri[:, :128] = (np.arange(128)[None, :] >= k)
    tri[:, 128:] = ((896 + k) <= (FAR + np.arange(HALO)[None, :]))
    shared["tri"] = tri
    shared["ident"] = np.eye(128, dtype=np.float32)
    vbase = np.zeros((128, NV), np.float32)

    def put(name, arr, off=0):
        arr = np.asarray(arr, np.float32)
        vbase[:, VC[name] + off:VC[name] + off + arr.shape[1]] = arr
    for l in range(2):
        put("b_mod", _fm(f("b_mod")[l]), l * 96)
        put("g_mix", _fm(f("norm_mix_g")[l]), l * 16)
        put("g_ffn", _fm(f("norm_ffn_g")[l]), l * 16)
        wfc = f("ffn_w_conv")[l]
        put("w_fc", wfc.reshape(3, NJ, 128).transpose(2, 1, 0).reshape(128, NJ * 3), l * NJ * 3)
        put("b_fc", _fm(f("ffn_b_conv")[l]), l * NJ)
    put("g_fin", _fm(f("final_norm_g")))
    put("g_q", _fm(f("q_norm_g")[0]))
    put("g_kv", _fm(f("kv_norm_g")[0]))
    put("b_pw1", _fm(f("conv_b_pw1")[0]))
    wdw = f("conv_w_dw")[0]
    put("w_dw", wdw.reshape(31, KC, 128).transpose(2, 1, 0).reshape(128, KC * 31))
    put("b_dw", _fm(f("conv_b_dw")[0]))
    put("ln_g", _fm(f("conv_ln_g")[0]))
    put("ln_b", _fm(f("conv_ln_b")[0]))
    put("b_pw2", _fm(f("conv_b_pw2")[0]))
    vbase[:, VC["pm16"]] = np.arange(128) % 32
    vbase[:, VC["eps"]] = 1e-6
    vbase[:, VC["tiny"]] = 1e-30
    pt = np.asarray(inp["page_table"], np.int32)
    c_prompt, c_sample = f("c_prompt"), f("c_sample")
    sconv, sffn = f("state_conv")[0], f("state_ffn")
    in_maps = []
    for c in cores:
        b, half = c // 2, c % 2
        m = dict(shared)
        xT = np.zeros((D, NKEY), np.float32)
        if half == 1:
            xT[:, :1024] = x_prompt[b, :1024].T
        xT[:, 1024:] = x_prompt[b, half * 1024:(half + 1) * 1024].T
        m["xT"] = xT
        ss = slice(c * NS, (c + 1) * NS)
        m["xsT"] = np.ascontiguousarray(x_sample[ss, 0].T)
        m["cT"] = np.ascontiguousarray(np.concatenate([c_prompt[b:b + 1], c_sample[ss]], axis=0).T)
        v = vbase.copy()
        v[:, VC["hm"]] = float(half)
        v[:, VC["cb"]] = 0.0 if half == 1 else -30000.0
        m["vecs"] = v
        m["rope"] = _rope_tables(half)
        ptc = pt[ss]
        m["ptrep"] = np.ascontiguousarray(
            ptc.reshape(NS, 16, 4)[:, :, :, None].repeat(32, axis=3).reshape(NS * 16, 128).T).astype(np.int32)
        m["state_conv"] = np.ascontiguousarray(sconv[ss].transpose(2, 0, 1))
        m["state_ffn"] = np.ascontiguousarray(sffn[:, ss].transpose(0, 3, 1, 2))
        in_maps.append(m)
    return in_maps


def assemble(res, cores, out=None):
    B, SEQ, DEC = 4, 2048, 128
    if out is None:
        out = [np.zeros((B, SEQ, D), np.float32), np.zeros((DEC, 1, D), np.float32),
               np.zeros((1, B, SEQ, 512), np.float32), np.zeros((1, B, SEQ, 64), np.float32),
               np.zeros((1, DEC, 1, 512), np.float32), np.zeros((1, DEC, 1, 64), np.float32),
               np.zeros((1, B, 30, D), np.float32), np.zeros((1, DEC, 30, D), np.float32),
               np.zeros((2, B, 2, DFF), np.float32), np.zeros((2, DEC, 2, DFF), np.float32)]
    for r, c in zip(res, cores):
        b, half = c // 2, c % 2
        ts_ = slice(half * 1024, (half + 1) * 1024)
        ss = slice(c * NS, (c + 1) * NS)
        out[0][b, ts_] = r["yT"].T
        out[1][ss, 0] = r["ysT"].T
        out[2][0, b, ts_] = r["ckv_o"].T
        out[3][0, b, ts_] = r["kpe_o"].T
        out[4][0, ss, 0] = r["ckv_so"].T
        out[5][0, ss, 0] = r["kpe_so"].T
        if half == 1:
            out[6][0, b] = r["conv_o"].T
            out[8][:, b] = r["ffn_o"].transpose(0, 2, 1)
        out[7][0, ss] = r["conv_so"].transpose(1, 2, 0)
        out[9][:, ss] = r["ffn_so"].transpose(0, 2, 3, 1)
    return tuple(out)


def kernel(**inputs):
    cores = list(range(8))
    nc = build_nc()
    in_maps = prepare(inputs, cores)
    res = run_bass_kernel_spmd(nc, in_maps, core_ids=cores)
    return assemble(res.results, cores)
```

```python
import contextlib
import math
import numpy as np
import concourse.bass as bass
import concourse.mybir as mybir
from concourse.bass_utils import run_bass_kernel_spmd

F32 = mybir.dt.float32
BF16 = mybir.dt.bfloat16
I32 = mybir.dt.int32
AF = mybir.ActivationFunctionType
ALU = mybir.AluOpType
AX = mybir.AxisListType

ENGS = ("pe", "act", "dve", "pool", "sp")

D = 2048
KC = 16
HALO = 34
OWN = 1024
NPR = OWN + HALO
NS = 16
T = NPR + NS
FAR = 1024 - HALO
NKEY = 2048
CKW = NKEY + NS
TILES = [(0, 358), (358, 716), (716, 1074)]
CTXT = [(0, 330), (330, 660), (660, 990)]
DFF = 5632
NJ = 44
NH = 16
SCALE = 1.0 / math.sqrt(192.0)
NPOOL = 10240
FGROUPS = [(0, 8), (8, 16), (16, 24), (24, 32), (32, 40), (40, 44)]
QT = [(0, HALO), (HALO, HALO + 512), (HALO + 512, NPR)]

VC = {}
_o = 0
for _n, _w in [("b_mod", 192), ("g_mix", 32), ("g_ffn", 32), ("g_fin", 16), ("g_q", 4), ("g_kv", 4),
               ("b_pw1", 32), ("w_dw", 31 * 16), ("b_dw", 16), ("ln_g", 16), ("ln_b", 16), ("b_pw2", 16),
               ("w_fc", 2 * 44 * 3), ("b_fc", 2 * 44), ("hm", 1), ("cb", 1), ("zero", 1), ("pm16", 1),
               ("eps", 1), ("tiny", 1)]:
    VC[_n] = _o
    _o += _w
NV = _o


class Res:
    __slots__ = ("w", "r")

    def __init__(self):
        self.w = None
        self.r = []


def alias(olds):
    n = Res()
    for o in olds:
        if o.w is not None:
            n.r.append(o.w)
        n.r.extend(o.r)
    return n


class Ins:
    __slots__ = ("eng", "fn", "deps", "sig", "is_dma", "dsem", "dval", "has_dep")

    def __init__(self, eng, fn, is_dma):
        self.eng = eng
        self.fn = fn
        self.deps = []
        self.sig = 0
        self.is_dma = is_dma
        self.dsem = None
        self.dval = 0
        self.has_dep = False


class Sched:
    def __init__(self, nc, n_dma_sems=8):
        self.nc = nc
        self.q = {e: [] for e in ENGS}
        self.n_dma_sems = n_dma_sems

    def _add(self, eng, fn, reads, writes, is_dma):
        ins = Ins(eng, fn, is_dma)
        deps = []
        for r in reads:
            if r.w is not None:
                deps.append(r.w)
        for w in writes:
            if w.w is not None:
                deps.append(w.w)
            deps.extend(w.r)
        seen = set()
        for d in deps:
            if d is ins or id(d) in seen:
                continue
            seen.add(id(d))
            if (not d.is_dma) and d.eng == "pe" and eng == "pe" and not is_dma:
                continue
            ins.deps.append(d)
            d.has_dep = True
        for r in reads:
            r.r.append(ins)
        for w in writes:
            w.w = ins
            w.r = []
        self.q[eng].append(ins)
        return ins

    def op(self, eng, fn, reads=(), writes=()):
        return self._add(eng, fn, reads, writes, False)

    def dma(self, eng, fn, reads=(), writes=()):
        return self._add(eng, fn, reads, writes, True)

    def emit(self):
        nc = self.nc
        with contextlib.ExitStack() as st:
            esem = {e: st.enter_context(nc.semaphore("s_" + e)) for e in ENGS}
            dsem = {e: [st.enter_context(nc.semaphore("d_%s%d" % (e, i))) for i in range(self.n_dma_sems)]
                    for e in ("sp", "pool", "act")}
            for e in ENGS:
                cnt = 0
                dcnt = [0] * self.n_dma_sems
                rr = 0
                for ins in self.q[e]:
                    if ins.is_dma:
                        k = rr % self.n_dma_sems
                        rr += 1
                        dcnt[k] += 16
                        ins.dsem = (e, k)
                        ins.dval = dcnt[k]
                    elif ins.has_dep:
                        cnt += 1
                        ins.sig = cnt
            block = st.enter_context(nc.Block())
            engobj = {"pe": "tensor", "act": "scalar", "dve": "vector", "pool": "gpsimd", "sp": "sync"}

            def run_engine(e, eng):
                known = {}

                def wait(key, sem, val):
                    if known.get(key, 0) >= val:
                        return
                    known[key] = val
                    eng.wait_ge(sem, val)

                last_dma = {}
                for ins in self.q[e]:
                    for d in ins.deps:
                        if d.is_dma:
                            wait(("d",) + d.dsem, dsem[d.dsem[0]][d.dsem[1]], d.dval)
                        else:
                            wait(("e", d.eng), esem[d.eng], d.sig)
                    if ins.is_dma:
                        k = ins.dsem[1]
                        if ins.dval > 16:
                            wait(("d", e, k), dsem[e][k], ins.dval - 16)
                        ins.fn(eng).then_inc(dsem[e][k], 16)
                        last_dma[k] = ins.dval
                    else:
                        r = ins.fn(eng)
                        if ins.sig:
                            r.then_inc(esem[e], 1)
                for k, v in last_dma.items():
                    wait(("d", e, k), dsem[e][k], v)

            for e in ENGS:
                if not self.q[e]:
                    continue

                def body(eng, e=e):
                    run_engine(e, eng)
                getattr(block, engobj[e])(body)


def bcast(ap, axis, n):
    l = [list(x) for x in ap.ap]
    l.insert(axis, [0, n])
    return bass.AP(ap.tensor, ap.offset, l)


def build_nc(stop_after=None):
    nc = bass.Bass("TRN2", target_bir_lowering=False)
    S = Sched(nc)

    def din(name, shape, dt=F32):
        return nc.dram_tensor(name, list(shape), dt, kind="ExternalInput").ap()

    def dout(name, shape, dt=F32):
        return nc.dram_tensor(name, list(shape), dt, kind="ExternalOutput").ap()

    xT = din("xT", [D, NKEY])
    xsT = din("xsT", [D, NS])
    cT = din("cT", [D, 17])
    vecs_d = din("vecs", [128, NV])
    rope_d = din("rope", [64, 2, CKW])
    tri_d = din("tri", [128, 128 + HALO])
    ident_d = din("ident", [128, 128])
    ptrep_d = din("ptrep", [128, 256], I32)
    cckv = din("cache_ckv", [NPOOL * 32, 4 * 512])
    ckpe = din("cache_kpe", [NPOOL * 32, 4 * 64])
    sconv = din("state_conv", [D, NS, 30])
    sffn = din("state_ffn", [2, DFF, NS, 2])
    w_mod = din("w_mod", [2, D, 6 * D])
    wq_a = din("wq_a", [D, 512])
    wq_b = din("wq_b", [512, NH * 256])
    wkv_a = din("wkv_a", [D, 512 + 128])
    w_uk = din("w_uk", [512, NH * 128])
    w_ukT = din("w_ukT", [128, NH * 512])
    w_uv = din("w_uv", [512, NH * 128])
    w_o = din("w_o", [D, D])
    w_pw1 = din("w_pw1", [D, 2 * D])
    w_pw2 = din("w_pw2", [D, D])
    w_gate = din("w_gate", [2, D, DFF])
    w_up = din("w_up", [2, D, DFF])
    w_down = din("w_down", [2, DFF, D])
    yT = dout("yT", [D, OWN])
    ysT = dout("ysT", [D, NS])
    ckv_o = dout("ckv_o", [512, OWN])
    kpe_o = dout("kpe_o", [64, OWN])
    ckv_so = dout("ckv_so", [512, NS])
    kpe_so = dout("kpe_so", [64, NS])
    conv_o = dout("conv_o", [D, 30])
    conv_so = dout("conv_so", [D, NS, 30])
    ffn_o = dout("ffn_o", [2, DFF, 2])
    ffn_so = dout("ffn_so", [2, DFF, NS, 2])

    with contextlib.ExitStack() as st:
        def sb(name, shape, dt):
            return st.enter_context(nc.sbuf_tensor("sb_" + name, list(shape), dt))

        xres = sb("xres", [128, KC, T], F32)
        hbuf = sb("hbuf", [128, KC * T], BF16)
        arA = sb("arA", [128, KC * T], BF16)
        arB = sb("arB", [128, 4 * T], BF16)
        arC = sb("arC", [128, 3328], F32)
        NWB = 2
        wb = [sb("wb%d" % i, [128, 4096], BF16) for i in range(NWB)]
        vecs = sb("vecs", [128, NV], F32)
        modv = sb("modv", [128, 6, KC, 17], F32)
        tri = sb("tri_f", [128, 128 + HALO], F32)
        trib = sb("trib", [128, 128 + HALO], BF16)
        ident = sb("ident", [128, 128], BF16)
        identf = sb("identf", [128, 128], F32)
        ones = sb("ones", [128, 128], BF16)
        csb = sb("csb", [128, KC, 17], F32)
        csbf = sb("csbf", [128, KC, 17], BF16)
        rstd = [sb("rstd%d" % i, [128, 384], F32) for i in range(2)]
        tmpf = [sb("tmpf%d" % i, [128, 384], F32) for i in range(2)]
        gbuf = [sb("gbuf%d" % i, [128, 384], F32) for i in range(2)]
        sfst = sb("sfst", [128, 8, NS, 2], F32)
        pfo = sb("pfo", [128, NJ, 2], F32)
        small = sb("small", [128, 512], F32)
        qs_n = sb("qs_n", [128, NH, NS], BF16)
        qs_p = sb("qs_p", [64, NH, NS], BF16)
        osT = sb("osT", [128, NH, NS], BF16)
        idx_i = sb("idx_i", [128, 256], I32)
        idx_f = sb("idx_f", [128, 256], F32)
        PS = [st.enter_context(nc.psum_tensor("ps%d" % i, [128, 512], F32)) for i in range(8)]
        rPS = [Res() for _ in range(8)]

        r_xres = [Res() for _ in range(3)]
        r_vecs, r_tri, r_ident, r_ones, r_csb, r_csbf = (Res() for _ in range(6))
        r_modv = [Res() for _ in range(6)]
        r_rstd = [Res(), Res()]
        r_tmpf = [Res(), Res()]
        r_gbuf = [Res(), Res()]
        r_wb = [Res() for _ in range(NWB)]
        r_wbB = [Res() for _ in range(NWB)]
        r_sfst, r_pfo, r_small = Res(), Res(), Res()
        r_qs, r_osT, r_idx = Res(), Res(), Res()
        cnt = {"wb": 0, "tmp": 0, "rstd": 0, "ps": 0, "ps45": 0}
        h3 = hbuf[:, :].rearrange("p (c t) -> p c t", c=KC)

        def V(name, off=0, n=1):
            c = VC[name] + off
            return vecs[:, c:c + n]

        def wload(view, shape):
            k = cnt["wb"] % NWB
            cnt["wb"] += 1
            n = int(np.prod(shape[1:]))
            assert n <= 4096
            dst = wb[k][:, 0:n]
            if len(shape) == 3:
                dst = dst.rearrange("p (a b) -> p a b", a=shape[1])
            elif len(shape) == 4:
                dst = dst.rearrange("p (a b c) -> p a b c", a=shape[1], b=shape[2])
            S.dma("pool", lambda e: e.dma_start(out=dst, in_=view), writes=[r_wb[k], r_wbB[k]])
            return dst, [r_wb[k], r_wbB[k]]

        def wload2(viewA, viewB, inner=KC):
            k = cnt["wb"] % NWB
            cnt["wb"] += 1
            wt = wb[k][:, 0:4096].rearrange("p (a c n) -> p a c n", a=2, c=inner)
            S.dma("pool", lambda e: e.dma_start(out=wt[:, 0], in_=viewA), writes=[r_wb[k]])
            S.dma("pool", lambda e: e.dma_start(out=wt[:, 1], in_=viewB), writes=[r_wbB[k]])
            return wt, [r_wb[k], r_wbB[k]]

        def new_tmp():
            k = cnt["tmp"] % 2
            cnt["tmp"] += 1
            return tmpf[k], r_tmpf[k]

        def new_rstd():
            k = cnt["rstd"] % 2
            cnt["rstd"] += 1
            return rstd[k], r_rstd[k]

        def new_ps(pool=(0, 1, 2, 3)):
            key = "ps" if len(pool) == 4 else "ps45"
            k = pool[cnt[key] % len(pool)]
            cnt[key] += 1
            return PS[k], rPS[k]

        def mm(out, lhsT, rhs, start, stop, reads, writes):
            S.op("pe", lambda e: e.matmul(out, lhsT=lhsT, rhs=rhs, start=start, stop=stop),
                 reads=reads, writes=writes)

        def tp(out, in_, idn, reads, writes):
            S.op("pe", lambda e: e.transpose(out, in_, idn), reads=reads, writes=writes)

        def act(out, in_, func, reads, writes, bias=None, scale=None):
            kw = {}
            if bias is not None:
                kw["bias"] = bias
            if scale is not None:
                kw["scale"] = scale
            S.op("act", lambda e: e.activation(out=out, in_=in_, func=func, **kw), reads=reads, writes=writes)

        def tt(out, in0, in1, op, reads, writes, eng="dve"):
            S.op(eng, lambda e: e.tensor_tensor(out=out, in0=in0, in1=in1, op=op), reads=reads, writes=writes)

        def ts(out, in0, s1, s2, op0, op1, reads, writes, eng="dve"):
            if op1 is None:
                S.op(eng, lambda e: e.tensor_scalar(out=out, in0=in0, scalar1=s1, scalar2=None, op0=op0),
                     reads=reads, writes=writes)
            else:
                S.op(eng, lambda e: e.tensor_scalar(out=out, in0=in0, scalar1=s1, scalar2=s2, op0=op0, op1=op1),
                     reads=reads, writes=writes)

        def stt(out, in0, sc, in1, op0, op1, reads, writes):
            S.op("dve", lambda e: e.scalar_tensor_tensor(out=out, in0=in0, scalar=sc, in1=in1, op0=op0, op1=op1),
                 reads=reads, writes=writes)

        def recip(out, in_, reads, writes):
            S.op("dve", lambda e: e.reciprocal(out=out, in_=in_), reads=reads, writes=writes)

        def sp_dma(out, in_, reads=(), writes=()):
            S.dma("sp", lambda e: e.dma_start(out=out, in_=in_), reads=reads, writes=writes)

        def sum_bcast(src3, nchunks, n, reads, ps, rps):
            for c in range(nchunks):
                mm(ps[:, 0:n], ones[:, :], src3[:, c, 0:n], c == 0, c == nchunks - 1, list(reads) + [r_ones], [rps])

        def rsqrt_from_ps(ps, rps, n, inv_n):
            rt, rr = new_rstd()
            act(rt[:, 0:n], ps[:, 0:n], AF.Sqrt, [rps, r_vecs], [rr], bias=V("eps"), scale=inv_n)
            recip(rt[:, 0:n], rt[:, 0:n], [rr], [rr])
            return rt, rr

        sp_dma(vecs[:], vecs_d, writes=[r_vecs])
        sp_dma(tri[:], tri_d, writes=[r_tri])
        S.op("dve", lambda e: e.tensor_copy(out=trib[:], in_=tri[:]), reads=[r_tri], writes=[r_tri])
        sp_dma(identf[:], ident_d, writes=[r_ident])
        S.op("dve", lambda e: e.tensor_copy(out=ident[:], in_=identf[:]), reads=[r_ident], writes=[r_ident])
        S.op("dve", lambda e: e.memset(ones[:], 1.0), writes=[r_ones])
        sp_dma(csb[:], cT.rearrange("(c p) s -> p c s", p=128), writes=[r_csb])
        act(csbf[:], csb[:], AF.Silu, [r_csb], [r_csbf])
        xv = xT.rearrange("(c p) t -> p c t", p=128)
        for i, (a, b) in enumerate(TILES):
            bp = min(b, NPR)
            sp_dma(xres[:, :, a:bp], xv[:, :, FAR + a:FAR + bp], writes=[r_xres[i]])
        sp_dma(xres[:, :, NPR:T], xsT.rearrange("(c p) s -> p c s", p=128), writes=[r_xres[2]])
        sp_dma(idx_i[:], ptrep_d, writes=[r_idx])
        S.op("dve", lambda e: e.tensor_copy(out=idx_f[:], in_=idx_i[:]), reads=[r_idx], writes=[r_idx])
        ts(idx_f[:], idx_f[:], 32.0, V("pm16"), ALU.mult, ALU.add, [r_idx, r_vecs], [r_idx])
        S.op("dve", lambda e: e.tensor_copy(out=idx_i[:], in_=idx_f[:]), reads=[r_idx], writes=[r_idx])

        def mod_chunks(l, wt, rw, oc0, nch):
            for m in range(nch):
                oc = oc0 + m
                ps, rps = new_ps()
                for kc in range(KC):
                    mm(ps[:, 0:17], wt[:, kc, m * 128:(m + 1) * 128], csbf[:, kc, :], kc == 0, kc == KC - 1,
                       rw + [r_csbf], [rps])
                act(modv[:, oc // 16, oc % 16, :], ps[:, 0:17], AF.Identity, [rps, r_vecs], [r_modv[oc // 16]],
                    bias=V("b_mod", l * 96 + oc))

        def mod_finish(l, v):
            gname = "g_mix" if v == 1 else "g_ffn"
            g = bcast(V(gname, l * 16, 16), 2, 17)
            stt(modv[:, v], modv[:, v], 1.0, g, ALU.add, ALU.mult, [r_modv[v], r_vecs], [r_modv[v]])

        def mod_tile_wb(l, t):
            wv = w_mod[l].rearrange("(c p) n -> p c n", p=128)
            wt, rw = wload(wv[:, :, t * 256:(t + 1) * 256], [128, KC, 256])
            mod_chunks(l, wt, rw, 2 * t, 2)

        def mod_tile_slot(l, oc0, nch, slot3, r_slot):
            wv = w_mod[l].rearrange("(c p) n -> p c n", p=128)
            S.dma("pool", lambda e: e.dma_start(out=slot3, in_=wv[:, :, oc0 * 128:(oc0 + nch) * 128]),
                  writes=[r_slot])
            mod_chunks(l, slot3, [r_slot], oc0, nch)

        def modulation(l):
            for t in range(48):
                mod_tile_wb(l, t)
            mod_finish(l, 1)
            mod_finish(l, 4)

        def norm_h(vA, vB, sq3, r_sq, dst3, r_dst):
            for i, (a, b) in enumerate(TILES):
                n = b - a
                bp = min(b, NPR)
                npc = bp - a
                act(sq3[:, :, 0:n], xres[:, :, a:b], AF.Square, [r_xres[i]], [r_sq])
                ps, rps = new_ps((4, 5))
                sum_bcast(sq3, KC, n, [r_sq], ps, rps)
                rt, rr = rsqrt_from_ps(ps, rps, n, 1.0 / D)
                for kc in range(KC):
                    t_, rt_ = new_tmp()
                    stt(t_[:, 0:npc], xres[:, kc, a:bp], modv[:, vA, kc, 0:1], rt[:, 0:npc], ALU.mult, ALU.mult,
                        [r_xres[i], r_modv[vA], rr], [rt_])
                    act(dst3[:, kc, a:bp], t_[:, 0:npc], AF.Identity, [rt_, r_modv[vB]], [r_dst[i]],
                        bias=modv[:, vB, kc, 0:1])
                if b > NPR:
                    sm = small[:, 0:KC * NS].rearrange("p (c s) -> p c s", c=KC)
                    tt(sm, xres[:, :, NPR:T], bcast(rt[:, npc:npc + NS], 1, KC), ALU.mult, [r_xres[i], rr], [r_small])
                    tt(sm, sm, modv[:, vA, :, 1:17], ALU.mult, [r_small, r_modv[vA]], [r_small])
                    tt(dst3[:, :, NPR:T], sm, modv[:, vB, :, 1:17], ALU.add, [r_small, r_modv[vB]], [r_dst[i]])

        def x_update(oc, i, ps, rps, vG, bias_col=None):
            a, b = TILES[i]
            bp = min(b, NPR)
            npc = bp - a
            if bias_col is None:
                stt(xres[:, oc, a:bp], ps[:, 0:npc], modv[:, vG, oc, 0:1], xres[:, oc, a:bp], ALU.mult, ALU.add,
                    [rps, r_modv[vG], r_xres[i]], [r_xres[i]])
            else:
                t_, rt_ = new_tmp()
                ts(t_[:, 0:npc], ps[:, 0:npc], bias_col, modv[:, vG, oc, 0:1], ALU.add, ALU.mult,
                   [rps, r_vecs, r_modv[vG]], [rt_])
                tt(xres[:, oc, a:bp], xres[:, oc, a:bp], t_[:, 0:npc], ALU.add, [rt_, r_xres[i]], [r_xres[i]])
            if b > NPR:
                sm = small[:, 256:256 + NS]
                if bias_col is None:
                    tt(sm, ps[:, npc:npc + NS], modv[:, vG, oc, 1:17], ALU.mult, [rps, r_modv[vG]], [r_small])
                else:
                    stt(sm, ps[:, npc:npc + NS], bias_col, modv[:, vG, oc, 1:17], ALU.add, ALU.mult,
                        [rps, r_modv[vG], r_vecs], [r_small])
                tt(xres[:, oc, NPR:T], xres[:, oc, NPR:T], sm, ALU.add, [r_small, r_xres[i]], [r_xres[i]])

        modulation(0)
        ckvT = arA[:, 0:4 * CKW].rearrange("p (c t) -> p c t", c=4)
        kpeT = arA[0:64, 4 * CKW:5 * CKW]
        sqA = arA[:, 5 * CKW:5 * CKW + KC * 358].rearrange("p (c t) -> p c t", c=KC)
        r_ckvT, r_kpeT, r_sqA = Res(), Res(), Res()
        qa = arB[:, 0:4 * T].rearrange("p (c t) -> p c t", c=4)
        r_qa = Res()
        kvf = arC[:, 0:4 * 358].rearrange("p (c t) -> p c t", c=4)
        sq4 = arC[:, 1432:1432 + 716].bitcast(BF16).rearrange("p (c t) -> p c t", c=4)
        kstg = arC[0:64, 2148:2148 + 358]
        ropes = arC[0:64, 2506:2506 + 716].rearrange("p (a t) -> p a t", a=2)
        r_kvf, r_sq4, r_kstg, r_ropes = Res(), Res(), Res(), Res()

        def proj4(wts, src3, rsrc, n, gname, out_bf, r_out, out_f32_dma=None):
            for c in range(4):
                wt, rw, co = wts[c]
                ps, rps = new_ps()
                for kc in range(KC):
                    mm(ps[:, 0:n], wt[:, kc, co:co + 128], src3[:, kc, 0:n], kc == 0, kc == KC - 1, rw + [rsrc], [rps])
                act(kvf[:, c, 0:n], ps[:, 0:n], AF.Copy, [rps], [r_kvf])
            act(sq4[:, :, 0:n], kvf[:, :, 0:n], AF.Square, [r_kvf], [r_sq4])
            ps, rps = new_ps((4, 5))
            sum_bcast(sq4, 4, n, [r_sq4], ps, rps)
            rt, rr = rsqrt_from_ps(ps, rps, n, 1.0 / 512)
            for c in range(4):
                stt(kvf[:, c, 0:n], kvf[:, c, 0:n], V(gname, c), rt[:, 0:n], ALU.mult, ALU.mult,
                    [r_kvf, rr, r_vecs], [r_kvf])
            act(out_bf, kvf[:, :, 0:n], AF.Copy, [r_kvf], [r_out])

        wkv_v = wkv_a.rearrange("(c p) n -> p c n", p=128)
        wqa_v = wq_a.rearrange("(c p) n -> p c n", p=128)

        def kv_path(src3, rsrc, n, rel0):
            w0, rw0 = wload(wkv_v[:, :, 0:256], [128, KC, 256])
            w1, rw1 = wload(wkv_v[:, :, 256:512], [128, KC, 256])
            wts = [(w0, rw0, 0), (w0, rw0, 128), (w1, rw1, 0), (w1, rw1, 128)]
            proj4(wts, src3, rsrc, n, "g_kv", ckvT[:, :, rel0:rel0 + n], r_ckvT)
            lo, hi = max(rel0, 1024), min(rel0 + n, NKEY)
            if hi > lo:
                sp_dma(ckv_o.rearrange("(c p) t -> p c t", p=128)[:, :, lo - 1024:hi - 1024],
                       kvf[:, :, lo - rel0:hi - rel0], reads=[r_kvf])
            if rel0 + n > NKEY:
                so = NKEY - rel0
                sp_dma(ckv_so.rearrange("(c p) t -> p c t", p=128), kvf[:, :, so:so + NS], reads=[r_kvf])
            wp, rwp = wload(wkv_v[:, :, 512:640], [128, KC, 128])
            sp_dma(ropes[:, :, 0:n], rope_d[:, :, rel0:rel0 + n], writes=[r_ropes])
            psa, rpa = new_ps()
            psb, rpb = new_ps()
            for kc in range(KC):
                mm(psa[0:64, 0:n], wp[:, kc, 0:64], src3[:, kc, 0:n], kc == 0, kc == KC - 1, rwp + [rsrc], [rpa])
            for kc in range(KC):
                mm(psb[0:64, 0:n], wp[:, kc, 64:128], src3[:, kc, 0:n], kc == 0, kc == KC - 1, rwp + [rsrc], [rpb])
            t1, rt1 = new_tmp()
            t2, rt2 = new_tmp()
            tt(t1[0:64, 0:n], psa[0:64, 0:n], ropes[:, 0, 0:n], ALU.mult, [rpa, r_ropes], [rt1])
            tt(t2[0:64, 0:n], psb[0:64, 0:n], ropes[:, 1, 0:n], ALU.mult, [rpb, r_ropes], [rt2])
            tt(kstg[:, 0:n], t1[0:64, 0:n], t2[0:64, 0:n], ALU.add, [rt1, rt2], [r_kstg])
            act(kpeT[:, rel0:rel0 + n], kstg[:, 0:n], AF.Copy, [r_kstg], [r_kpeT])
            if hi > lo:
                sp_dma(kpe_o[:, lo - 1024:hi - 1024], kstg[:, lo - rel0:hi - rel0], reads=[r_kstg])
            if rel0 + n > NKEY:
                so = NKEY - rel0
                sp_dma(kpe_so, kstg[:, so:so + NS], reads=[r_kstg])

        xctx = hbuf[:, 0:2 * KC * 330].bitcast(F32).rearrange("p (c t) -> p c t", c=KC)
        hctx = hbuf[:, 2 * KC * 330:3 * KC * 330].rearrange("p (c t) -> p c t", c=KC)
        r_xctx, r_hctx = Res(), Res()
        for (a, b) in CTXT:
            n = b - a
            sp_dma(xctx[:, :, 0:n], xv[:, :, a:b], writes=[r_xctx])
            act(sqA[:, :, 0:n], xctx[:, :, 0:n], AF.Square, [r_xctx], [r_sqA])
            ps, rps = new_ps((4, 5))
            sum_bcast(sqA, KC, n, [r_sqA], ps, rps)
            rt, rr = rsqrt_from_ps(ps, rps, n, 1.0 / D)
            for kc in range(KC):
                t_, rt_ = new_tmp()
                stt(t_[:, 0:n], xctx[:, kc, 0:n], modv[:, 1, kc, 0:1], rt[:, 0:n], ALU.mult, ALU.mult,
                    [r_xctx, r_modv[1], rr], [rt_])
                act(hctx[:, kc, 0:n], t_[:, 0:n], AF.Identity, [rt_, r_modv[0]], [r_hctx], bias=modv[:, 0, kc, 0:1])
            kv_path(hctx, r_hctx, n, a)
        r_h = [alias([r_xctx, r_hctx]) for _ in range(3)]
        norm_h(1, 0, sqA, r_sqA, h3, r_h)
        for i, (a, b) in enumerate(TILES):
            kv_path(h3[:, :, a:b], r_h[i], b - a, FAR + a)
        for i, (a, b) in enumerate(TILES):
            w0, rw0 = wload(wqa_v[:, :, 0:256], [128, KC, 256])
            w1, rw1 = wload(wqa_v[:, :, 256:512], [128, KC, 256])
            wts = [(w0, rw0, 0), (w0, rw0, 128), (w1, rw1, 0), (w1, rw1, 128)]
            proj4(wts, h3[:, :, a:b], r_h[i], b - a, "g_q", qa[:, :, a:b], r_qa)

        r_arC2 = alias([r_kvf, r_sq4, r_kstg, r_ropes])
        ropeL = arC[0:64, 0:2 * T].rearrange("p (a t) -> p a t", a=2)
        sp_dma(ropeL, rope_d[:, :, FAR:FAR + T], writes=[r_arC2])

        wqb_v = wq_b.rearrange("(c p) n -> p c n", p=128)
        wuk_v = w_uk.rearrange("(c p) n -> p c n", p=128)
        wuv_v = w_uv.rearrange("(c p) n -> p c n", p=128)
        wukT_v = w_ukT.rearrange("p (h c) -> p h c", h=NH)
        wo_v = w_o.rearrange("(h p) n -> p h n", p=128)

        def q_head(wt, rw, hh, cols_a, cols_b, out_n, out_p, r_out):
            base = hh * 256
            c0 = cols_a
            while c0 < cols_b:
                c1 = min(c0 + 358, cols_b)
                n = c1 - c0
                o0 = c0 - cols_a
                ps, rps = new_ps()
                for kc in range(4):
                    mm(ps[:, 0:n], wt[:, kc, base:base + 128], qa[:, kc, c0:c1], kc == 0, kc == 3,
                       rw + [r_qa], [rps])
                act(out_n[:, o0:o0 + n], ps[:, 0:n], AF.Copy, [rps], [r_out])
                psa, rpa = new_ps()
                psb, rpb = new_ps()
                for kc in range(4):
                    mm(psa[0:64, 0:n], wt[:, kc, base + 128:base + 192], qa[:, kc, c0:c1], kc == 0, kc == 3,
                       rw + [r_qa], [rpa])
                for kc in range(4):
                    mm(psb[0:64, 0:n], wt[:, kc, base + 192:base + 256], qa[:, kc, c0:c1], kc == 0, kc == 3,
                       rw + [r_qa], [rpb])
                t1, rt1 = new_tmp()
                t2, rt2 = new_tmp()
                tt(t1[0:64, 0:n], psa[0:64, 0:n], ropeL[:, 0, c0:c1], ALU.mult, [rpa, r_arC2], [rt1])
                tt(t2[0:64, 0:n], psb[0:64, 0:n], ropeL[:, 1, c0:c1], ALU.mult, [rpb, r_arC2], [rt2])
                tt(out_p[:, o0:o0 + n], t1[0:64, 0:n], t2[0:64, 0:n], ALU.add, [rt1, rt2], [r_out])
                c0 = c1

        r_hfree = alias(r_h)
        o = 0
        GS = 4
        NCK = 4
        ck, kp = [], []
        for i in range(NCK):
            ck.append(hbuf[:, o:o + GS * 512].rearrange("p (s c) -> p s c", s=GS))
            o += GS * 512
        for i in range(NCK):
            kp.append(hbuf[:, o:o + GS * 64].rearrange("p (s c) -> p s c", s=GS))
            o += GS * 64
        ckTs, kpTs, ptss = [], [], []
        for i in range(2):
            ckTs.append(hbuf[:, o:o + 4 * GS * 128].rearrange("p (c k) -> p c k", c=4))
            o += 4 * GS * 128
        for i in range(2):
            kpTs.append(hbuf[0:64, o:o + GS * 128])
            o += GS * 128
        for i in range(2):
            ptss.append(hbuf[:, o:o + GS * 16].rearrange("p (s h) -> p s h", s=GS))
            o += GS * 16
        qlat = hbuf[:, o:o + 1024].rearrange("p (c h s) -> p c h s", c=4, h=NH)
        o += 1024
        olatT = hbuf[:, o:o + 1024].rearrange("p (c h s) -> p c h s", c=4, h=NH)
        o += 1024
        xb16 = arC[0:16, 2176:2176 + 640].bitcast(BF16)
        cnew = xb16[:, 0:512]
        olb = xb16[:, 512:1024]
        ptn = xb16[:, 1024:1040]
        assert o <= KC * T
        r_ck = [alias([r_hfree]) for _ in range(NCK)]
        r_kp = [alias([r_hfree]) for _ in range(NCK)]
        r_ckTs = [alias([r_hfree]) for _ in range(2)]
        r_kpTs = [alias([r_hfree]) for _ in range(2)]
        r_ptss = [alias([r_hfree]) for _ in range(2)]
        r_qlat, r_olatT = alias([r_hfree]), alias([r_hfree])
        r_cnew, r_ptn, r_olb = (alias([r_kvf, r_sq4, r_kstg, r_ropes]) for _ in range(3))

        for g in range(4):
            wt, rw = wload(wqb_v[:, :, g * 1024:(g + 1) * 1024], [128, 4, 1024])
            for hh in range(4):
                h_ = g * 4 + hh
                q_head(wt, rw, hh, NPR, T, qs_n[:, h_, :], qs_p[:, h_, :], r_qs)
        for g in range(4):
            wt, rw = wload(wukT_v[:, g * 4:(g + 1) * 4, :], [128, 4, 512])
            for hh in range(4):
                h_ = g * 4 + hh
                ps, rps = new_ps()
                for c in range(4):
                    mm(ps[:, c * 16:(c + 1) * 16], wt[:, hh, c * 128:(c + 1) * 128], qs_n[:, h_, :], True, True,
                       rw + [r_qs], [rps])
                act(qlat[:, :, h_, :], ps[:, 0:64].rearrange("p (c s) -> p c s", c=4), AF.Copy, [rps], [r_qlat])
        ps, rps = new_ps()
        psb16 = ps[:, :].bitcast(BF16)
        for c in range(4):
            tp(psb16[0:16, c * 128:(c + 1) * 128], ckvT[:, c, NKEY:NKEY + NS], ident[:, :], [r_ckvT, r_ident], [rps])
        act(cnew, psb16[0:16, 0:512], AF.Copy, [rps], [r_cnew])

        cck_rows = cckv
        ckp_rows = ckpe
        psO, rpO = PS[6], rPS[6]
        psl, rpl = PS[7], rPS[7]
        NGS = 64 // GS

        def s_gather(g):
            cb_ = g % NCK
            S.dma("pool", lambda e: e.indirect_dma_start(
                out=ck[cb_].rearrange("p s c -> p (s c)"), out_offset=None, in_=cckv[:, :],
                in_offset=bass.IndirectOffsetOnAxis(ap=idx_i[:, g:g + 1], axis=0)),
                reads=[r_idx], writes=[r_ck[cb_]])
            S.dma("pool", lambda e: e.indirect_dma_start(
                out=kp[cb_].rearrange("p s c -> p (s c)"), out_offset=None, in_=ckpe[:, :],
                in_offset=bass.IndirectOffsetOnAxis(ap=idx_i[:, g:g + 1], axis=0)),
                reads=[r_idx], writes=[r_kp[cb_]])

        def s_transpose(g):
            cb_, tb = g % NCK, g % 2
            cT_, rcT_ = ckTs[tb], r_ckTs[tb]
            kT_, rkT_ = kpTs[tb], r_kpTs[tb]
            for s2 in range(GS // 2):
                ps, rps = new_ps()
                pb = ps[:, :].bitcast(BF16)
                for s in (2 * s2, 2 * s2 + 1):
                    for c in range(4):
                        o_ = (s - 2 * s2) * 512 + c * 128
                        tp(pb[:, o_:o_ + 128], ck[cb_][:, s, c * 128:(c + 1) * 128], ident[:, :],
                           [r_ck[cb_], r_ident], [rps])
                src = pb[:, 0:1024].rearrange("p (s c k) -> p c s k", s=2, c=4)
                dst = cT_[:, :, 2 * s2 * 128:(2 * s2 + 2) * 128].rearrange("p c (s k) -> p c s k", s=2)
                if s2 % 2 == 0:
                    act(dst, src, AF.Copy, [rps], [rcT_])
                else:
                    S.op("dve", lambda e, dst=dst, src=src: e.tensor_copy(out=dst, in_=src),
                         reads=[rps], writes=[rcT_])
            ps, rps = new_ps()
            pb = ps[:, :].bitcast(BF16)
            for s in range(GS):
                tp(pb[0:64, s * 128:(s + 1) * 128], kp[cb_][:, s, :], ident[:, :], [r_kp[cb_], r_ident], [rps])
            S.op("dve", lambda e, pb=pb, kT_=kT_: e.tensor_copy(out=kT_[:, :], in_=pb[0:64, 0:GS * 128]),
                 reads=[rps], writes=[rkT_])

        def s_scores(g):
            b_, tb = g // NGS, g % 2
            cT_, rcT_ = ckTs[tb], r_ckTs[tb]
            kT_, rkT_ = kpTs[tb], r_kpTs[tb]
            psS, rpS = new_ps((4, 5))
            for s in range(GS):
                for c in range(4):
                    mm(psS[:, s * 16:(s + 1) * 16], cT_[:, c, s * 128:(s + 1) * 128], qlat[:, c, :, b_],
                       c == 0, False, [rcT_, r_qlat], [rpS])
                mm(psS[:, s * 16:(s + 1) * 16], kT_[:, s * 128:(s + 1) * 128], qs_p[:, :, b_], False, True,
                   [rkT_, r_qs], [rpS])
            act(ptss[tb], psS[:, 0:GS * 16].rearrange("p (s h) -> p s h", s=GS), AF.Exp, [rpS], [r_ptss[tb]],
                scale=SCALE)

        def s_pv(g):
            b_, j, cb_, tb = g // NGS, g % NGS, g % NCK, g % 2
            for s in range(GS):
                mm(psO[0:16, 0:512], ptss[tb][:, s, :], ck[cb_][:, s, :], j == 0 and s == 0, False,
                   [r_ptss[tb], r_ck[cb_]], [rpO])
            for s in range(GS):
                mm(psl[0:16, 0:1], ptss[tb][:, s, :], ones[:, 0:1], j == 0 and s == 0, False,
                   [r_ptss[tb], r_ones], [rpl])
            if j < NGS - 1:
                return
            psN, rpN = new_ps((4, 5))
            for c in range(4):
                mm(psN[0:16, 0:16], ckvT[:, c, NKEY:NKEY + NS], qlat[:, c, :, b_], c == 0, False,
                   [r_ckvT, r_qlat], [rpN])
            mm(psN[0:16, 0:16], kpeT[:, NKEY:NKEY + NS], qs_p[:, :, b_], False, True, [r_kpeT, r_qs], [rpN])
            t_, rt_ = new_tmp()
            act(t_[0:16, 0:16], psN[0:16, 0:16], AF.Exp, [rpN], [rt_], scale=SCALE)
            ts(ptn, t_[0:16, 0:16], identf[0:16, b_:b_ + 1], None, ALU.mult, None, [rt_, r_ident], [r_ptn])
            mm(psO[0:16, 0:512], ptn, cnew, False, True, [r_ptn, r_cnew], [rpO])
            mm(psl[0:16, 0:1], ptn, ones[0:16, 0:1], False, True, [r_ptn, r_ones], [rpl])
            rl = small[0:16, 300:301]
            recip(rl, psl[0:16, 0:1], [rpl], [r_small])
            ts(olb, psO[0:16, 0:512], rl, None, ALU.mult, None, [rpO, r_small], [r_olb])
            ps, rps = new_ps()
            pb = ps[:, :].bitcast(BF16)
            for c in range(4):
                tp(pb[:, c * 16:(c + 1) * 16], olb[:, c * 128:(c + 1) * 128], ident[0:16, 0:16], [r_olb, r_ident], [rps])
            act(olatT[:, :, :, b_], pb[:, 0:64].rearrange("p (c h) -> p c h", c=4), AF.Copy, [rps], [r_olatT])

        NG = NS * NGS
        PF = NCK - 1
        for g in range(min(PF, NG)):
            s_gather(g)
        s_transpose(0)
        for g in range(NG):
            if g + PF < NG:
                s_gather(g + PF)
            s_scores(g)
            if g + 1 < NG:
                s_transpose(g + 1)
            s_pv(g)
        for g in range(4):
            wt, rw = wload(wuv_v[:, :, g * 512:(g + 1) * 512], [128, 4, 512])
            for hh in range(4):
                h_ = g * 4 + hh
                ps, rps = new_ps()
                for c in range(4):
                    mm(ps[:, 0:NS], wt[:, c, hh * 128:(hh + 1) * 128], olatT[:, c, h_, :], c == 0, c == 3,
                       rw + [r_olatT], [rps])
                act(osT[:, h_, :], ps[:, 0:NS], AF.Copy, [rps], [r_osT])

        r_hfree2 = alias(r_ck + r_kp + r_ckTs + r_kpTs + r_ptss + [r_qlat, r_olatT, r_cnew, r_ptn, r_olb])
        HS = 6244
        knT, vh, qn, qp, r_hd = [], [], [], [], []
        for i in range(2):
            o = i * HS
            knT.append(hbuf[:, o:o + 2048])
            vh.append(hbuf[:, o + 2048:o + 4096].rearrange("p (k d) -> p k d", k=16))
            qn.append(hbuf[:, o + 4096:o + 4096 + T])
            qp.append(hbuf[0:64, o + 4096 + T:o + 4096 + 2 * T])
            r_hd.append({k: alias([r_hfree2]) for k in ("kn", "v", "q")})
        ptb = [hbuf[:, 2 * HS + i * 512:2 * HS + (i + 1) * 512] for i in range(4)]
        r_ptb = [alias([r_hfree2]) for _ in range(4)]
        oT = arA[:, 5 * CKW:5 * CKW + 4 * T].rearrange("p (h t) -> p h t", h=4)
        r_oT = alias([r_sqA])
        npt = [0]
        mslotA = arC[:, 2176:2176 + 1024].bitcast(BF16).rearrange("p (c n) -> p c n", c=KC)
        r_mslotA = alias([r_cnew, r_ptn, r_olb])
        for g in range(4):
            wtq, rwq = wload(wqb_v[:, :, g * 1024:(g + 1) * 1024], [128, 4, 1024])
            wkv2, rwk = wload2(wuk_v[:, :, g * 512:(g + 1) * 512], wuv_v[:, :, g * 512:(g + 1) * 512], inner=4)
            wtk, wtv, rwv = wkv2[:, 0], wkv2[:, 1], rwk
            for hh in range(4):
                h_ = g * 4 + hh
                hb = h_ % 2
                R = r_hd[hb]
                q_head(wtq, rwq, hh, 0, NPR, qn[hb][:, 0:NPR], qp[hb][:, 0:NPR], R["q"])
                for kt in range(4):
                    ps, rps = new_ps()
                    for c in range(4):
                        mm(ps[:, 0:512], wtk[:, c, hh * 128:(hh + 1) * 128], ckvT[:, c, kt * 512:(kt + 1) * 512],
                           c == 0, c == 3, rwk + [r_ckvT], [rps])
                    if kt % 2 == 0:
                        act(knT[hb][:, kt * 512:(kt + 1) * 512], ps[:, 0:512], AF.Copy, [rps], [R["kn"]])
                    else:
                        S.op("dve", lambda e, hb=hb, kt=kt, ps=ps: e.tensor_copy(
                            out=knT[hb][:, kt * 512:(kt + 1) * 512], in_=ps[:, 0:512]), reads=[rps], writes=[R["kn"]])
                for k4 in range(4):
                    ps, rps = new_ps()
                    for kk in range(4):
                        kb = k4 * 4 + kk
                        for c in range(4):
                            mm(ps[:, kk * 128:(kk + 1) * 128], ckvT[:, c, kb * 128:(kb + 1) * 128],
                               wtv[:, c, hh * 128:(hh + 1) * 128], c == 0, c == 3, rwv + [r_ckvT], [rps])
                    dst = vh[hb][:, k4 * 4:(k4 + 1) * 4, :]
                    src = ps[:, 0:512].rearrange("p (k d) -> p k d", k=4)
                    if k4 % 2 == 0:
                        S.op("dve", lambda e, dst=dst, src=src: e.tensor_copy(out=dst, in_=src),
                             reads=[rps], writes=[R["v"]])
                    else:
                        act(dst, src, AF.Copy, [rps], [R["v"]])
                mod_tile_slot(1, 2 * h_, 1, mslotA, r_mslotA)
                pend = []

                def flush(keep):
                    while len(pend) > keep:
                        pend.pop(0)()

                for qi, (qa_, qb_) in enumerate(QT):
                    nq = qb_ - qa_
                    r0 = FAR + qa_
                    r1 = FAR + qb_
                    nkb = (r1 - 1) // 128 + 1
                    ob = (6, 7) if (qi + h_) % 2 == 0 else (4, 5)
                    psO, rpO = PS[ob[0]], rPS[ob[0]]
                    psL, rpL = PS[ob[1]], rPS[ob[1]]
                    for kb in range(nkb):
                        k0 = kb * 128
                        c0 = 0 if qa_ == 0 else max(0, k0 - r0)
                        nn = nq - c0
                        psS, rpS = new_ps()
                        mm(psS[:, 0:nn], knT[hb][:, k0:k0 + 128], qn[hb][:, qa_ + c0:qb_], True, False,
                           [R["kn"], R["q"]], [rpS])
                        mm(psS[:, 0:nn], kpeT[:, k0:k0 + 128], qp[hb][:, qa_ + c0:qb_], False, True,
                           [r_kpeT, R["q"]], [rpS])
                        pi = npt[0] % 4
                        npt[0] += 1
                        bias = V("cb") if kb < 8 else V("zero")
                        act(ptb[pi][:, 0:nn], psS[:, 0:nn], AF.Exp, [rpS, r_vecs], [r_ptb[pi]], bias=bias, scale=SCALE)
                        if qa_ == 0 and kb == 7:
                            tt(ptb[pi][:, 0:nn], ptb[pi][:, 0:nn], trib[:, 128:128 + HALO], ALU.mult,
                               [r_ptb[pi], r_tri], [r_ptb[pi]])
                        elif qa_ > 0 and k0 >= r0:
                            tt(ptb[pi][:, 0:128], ptb[pi][:, 0:128], trib[:, 0:128], ALU.mult,
                               [r_ptb[pi], r_tri], [r_ptb[pi]])

                        def pv(kb=kb, c0=c0, nn=nn, nq=nq, pi=pi, nkb=nkb, psO=psO, rpO=rpO, psL=psL, rpL=rpL,
                               qa_=qa_):
                            mm(psO[:, c0:nq], vh[hb][:, kb, :], ptb[pi][:, 0:nn], kb == 0, kb == nkb - 1,
                               [R["v"], r_ptb[pi]], [rpO])
                            mm(psL[:, c0:nq], ones[:, :], ptb[pi][:, 0:nn], kb == 0, kb == nkb - 1,
                               [r_ones, r_ptb[pi]], [rpL])
                            if kb < nkb - 1:
                                return
                            for (ha, hb2) in ((0, min(nq, 256)), (256, nq)):
                                if hb2 <= ha:
                                    continue
                                w_ = hb2 - ha
                                rt, rr = new_rstd()
                                ts(rt[:, 0:w_], psL[:, ha:hb2], V("tiny"), None, ALU.add, None, [rpL, r_vecs], [rr])
                                recip(rt[:, 0:w_], rt[:, 0:w_], [rr], [rr])
                                tt(oT[:, hh, qa_ + ha:qa_ + hb2], psO[:, ha:hb2], rt[:, 0:w_], ALU.mult,
                                   [rpO, rr], [r_oT])
                        pend.append(pv)
                        flush(2)
                flush(0)
                mod_tile_slot(1, 2 * h_ + 1, 1, mslotA, r_mslotA)
                S.op("dve", lambda e, hh=hh, h_=h_: e.tensor_copy(out=oT[:, hh, NPR:T], in_=osT[:, h_, :]),
                     reads=[r_osT], writes=[r_oT])
            for half in range(2):
                wt, rw = wload(wo_v[:, g * 4:(g + 1) * 4, half * 1024:(half + 1) * 1024], [128, 4, 1024])
                for m in range(8):
                    oc = half * 8 + m
                    for i, (a, b) in enumerate(TILES):
                        n = b - a
                        ps, rps = new_ps()
                        for hh in range(4):
                            mm(ps[:, 0:n], wt[:, hh, m * 128:(m + 1) * 128], oT[:, hh, a:b], hh == 0, hh == 3,
                               rw + [r_oT], [rps])
                        x_update(oc, i, ps, rps, 2)

        def ffn(l, prev_h, prev_A, hook=None):
            act1 = arA[:, 0:8 * T].rearrange("p (j t) -> p j t", j=8)
            r_act = alias(prev_A)
            sqF = arA[:, 8 * T:8 * T + KC * 358].rearrange("p (c t) -> p c t", c=KC)
            r_sqF = alias(prev_A)
            r_hh = [alias(prev_h) for _ in range(3)]
            norm_h(4, 3, sqF, r_sqF, h3, r_hh)
            wg_v = w_gate[l].rearrange("(c p) n -> p c n", p=128)
            wu_v = w_up[l].rearrange("(c p) n -> p c n", p=128)
            wd_v = w_down[l].rearrange("(j p) n -> p j n", p=128)
            sf_v = sffn[l].rearrange("(j p) s k -> p j s k", p=128)
            sfo_v = ffn_so[l].rearrange("(j p) s k -> p j s k", p=128)
            for (j0, j1) in FGROUPS:
                nj = j1 - j0
                sp_dma(sfst[:, 0:nj], sf_v[:, j0:j1], writes=[r_sfst])
                for j in range(j0, j1):
                    jj = j - j0
                    wt, rw = wload2(wg_v[:, :, j * 128:(j + 1) * 128], wu_v[:, :, j * 128:(j + 1) * 128])
                    wc = [V("w_fc", (l * NJ + j) * 3 + t_) for t_ in range(3)]
                    bc_ = V("b_fc", l * NJ + j)
                    for i, (a, b) in enumerate(TILES):
                        n = b - a
                        bp = min(b, NPR)
                        npc = bp - a
                        psG, rpG = new_ps()
                        psU, rpU = new_ps()
                        for kc in range(KC):
                            mm(psG[:, 0:n], wt[:, 0, kc, :], h3[:, kc, a:b], kc == 0, kc == KC - 1, rw + [r_hh[i]], [rpG])
                        for kc in range(KC):
                            mm(psU[:, 0:n], wt[:, 1, kc, :], h3[:, kc, a:b], kc == 0, kc == KC - 1, rw + [r_hh[i]], [rpU])
                        gi = i % 2
                        gb, rg = gbuf[gi], r_gbuf[gi]
                        act(gb[:, 2:2 + n], psG[:, 0:n], AF.Copy, [rpG], [rg])
                        if i == 0:
                            S.op("dve", lambda e, gb=gb: e.memset(gb[:, 0:2], 0.0), writes=[rg])
                            ts(gb[:, 2:2 + HALO], gb[:, 2:2 + HALO], V("hm"), None, ALU.mult, None, [rg, r_vecs], [rg])
                        if i < 2:
                            gn, rgn = gbuf[(i + 1) % 2], r_gbuf[(i + 1) % 2]
                            act(gn[:, 0:2], gb[:, n:n + 2], AF.Copy, [rg], [rgn])
                        else:
                            S.op("dve", lambda e, gb=gb, j=j, npc=npc: e.tensor_copy(
                                out=pfo[:, j, :], in_=gb[:, npc:npc + 2]), reads=[rg], writes=[r_pfo])
                        t_, rt_ = new_tmp()
                        ts(t_[:, 0:npc], gb[:, 2:2 + npc], wc[2], bc_, ALU.mult, ALU.add, [rg, r_vecs], [rt_])
                        stt(t_[:, 0:npc], gb[:, 1:1 + npc], wc[1], t_[:, 0:npc], ALU.mult, ALU.add, [rg, rt_, r_vecs], [rt_])
                        stt(t_[:, 0:npc], gb[:, 0:npc], wc[0], t_[:, 0:npc], ALU.mult, ALU.add, [rg, rt_, r_vecs], [rt_])
                        act(t_[:, 0:npc], t_[:, 0:npc], AF.Silu, [rt_], [rt_])
                        tt(act1[:, jj, a:bp], t_[:, 0:npc], psU[:, 0:npc], ALU.mult, [rt_, rpU], [r_act])
                        if b > NPR:
                            sm = small[:, 320:320 + NS]
                            gs = gb[:, 2 + npc:2 + npc + NS]
                            ts(sm, sfst[:, jj, :, 0], wc[0], bc_, ALU.mult, ALU.add, [r_sfst, r_vecs], [r_small])
                            stt(sm, sfst[:, jj, :, 1], wc[1], sm, ALU.mult, ALU.add, [r_sfst, r_small, r_vecs], [r_small])
                            stt(sm, gs, wc[2], sm, ALU.mult, ALU.add, [rg, r_small, r_vecs], [r_small])
                            act(sm, sm, AF.Silu, [r_small], [r_small])
                            tt(act1[:, jj, NPR:T], sm, psU[:, npc:npc + NS], ALU.mult, [r_small, rpU], [r_act])
                            S.op("dve", lambda e, jj=jj: e.tensor_copy(out=sfst[:, jj, :, 0], in_=sfst[:, jj, :, 1]),
                                 reads=[r_sfst], writes=[r_sfst])
                            S.op("dve", lambda e, jj=jj, gs=gs: e.tensor_copy(out=sfst[:, jj, :, 1], in_=gs),
                                 reads=[rg, r_sfst], writes=[r_sfst])
                    if hook is not None:
                        hook(j)
                sp_dma(sfo_v[:, j0:j1], sfst[:, 0:nj], reads=[r_sfst])
                for cg in range(4):
                    wt, rw = wload(wd_v[:, j0:j1, cg * 512:(cg + 1) * 512], [128, nj, 512])
                    for m in range(4):
                        oc = cg * 4 + m
                        for i, (a, b) in enumerate(TILES):
                            n = b - a
                            ps, rps = new_ps()
                            for jj in range(nj):
                                mm(ps[:, 0:n], wt[:, jj, m * 128:(m + 1) * 128], act1[:, jj, a:b], jj == 0, jj == nj - 1,
                                   rw + [r_act], [rps])
                            x_update(oc, i, ps, rps, 5)
            sp_dma(ffn_o[l].rearrange("(j p) k -> p j k", p=128), pfo[:, :, :], reads=[r_pfo])
            return r_hh, [r_act, r_sqF]

        prev_h = [r_hfree2] + [r_hd[i][k] for i in range(2) for k in ("kn", "v", "q")] + r_ptb
        mslotB = arB[:, 0:4096].rearrange("p (c n) -> p c n", c=KC)
        r_mslotB = alias([r_qa])

        def hook0(j):
            t0 = (j * 24) // NJ
            t1 = ((j + 1) * 24) // NJ
            for t in range(t0, t1):
                mod_tile_slot(1, 32 + 2 * t, 2, mslotB, r_mslotB)

        prev_h, prev_A = ffn(0, prev_h, [r_ckvT, r_kpeT, r_sqA, r_oT], hook=hook0)

        for t in range(40, 48):
            mod_tile_wb(1, t)
        mod_finish(1, 1)
        mod_finish(1, 4)
        r_hc = [alias(prev_h) for _ in range(3)]
        sqC = arC[:, 0:KC * 358 // 2].bitcast(BF16).rearrange("p (c t) -> p c t", c=KC)
        r_sqC = alias([r_arC2])
        norm_h(1, 0, sqC, r_sqC, h3, r_hc)
        ybf = arA[:, :].rearrange("p (c t) -> p c t", c=KC)
        r_y = [alias(prev_A) for _ in range(3)]
        UW = 30 + NPR
        ub = arC[:, 0:UW // 2].bitcast(BF16)
        cs = arC[:, 544:544 + 480].rearrange("p (s k) -> p s k", s=NS)
        cso = arC[:, 1024:1024 + 480].rearrange("p (s k) -> p s k", s=NS)
        r_ub, r_cs, r_cso = (alias([r_sqC]) for _ in range(3))
        dg = arB[:, 0:31 * 128].rearrange("p (k m) -> p k m", k=31)
        r_dg = alias([r_qa, r_mslotB])
        cpo = sb("cpo", [128, KC, 30], F32)
        r_cpo = Res()
        S.op("dve", lambda e: e.memset(ub[:, 0:30], 0.0), writes=[r_ub])
        w1_v = w_pw1.rearrange("(c p) n -> p c n", p=128)
        sc_v = sconv.rearrange("(c p) s k -> p c s k", p=128)
        cso_v = conv_so.rearrange("(c p) s k -> p c s k", p=128)
        for c in range(KC):
            wt, rw = wload2(w1_v[:, :, c * 128:(c + 1) * 128], w1_v[:, :, D + c * 128:D + (c + 1) * 128])
            sp_dma(cs, sc_v[:, c], writes=[r_cs])
            tt(dg, bcast(ident[:, :], 1, 31), bcast(V("w_dw", c * 31, 31), 2, 128), ALU.mult,
               [r_ident, r_vecs], [r_dg], eng="pool")
            us = small[:, 340:340 + NS]
            for i, (a, b) in enumerate(TILES):
                n = b - a
                bp = min(b, NPR)
                npc = bp - a
                psA, rpA = new_ps()
                psB, rpB = new_ps()
                for kc in range(KC):
                    mm(psA[:, 0:n], wt[:, 0, kc, :], h3[:, kc, a:b], kc == 0, kc == KC - 1, rw + [r_hc[i]], [rpA])
                for kc in range(KC):
                    mm(psB[:, 0:n], wt[:, 1, kc, :], h3[:, kc, a:b], kc == 0, kc == KC - 1, rw + [r_hc[i]], [rpB])
                t_, rt_ = new_tmp()
                act(t_[:, 0:n], psB[:, 0:n], AF.Sigmoid, [rpB, r_vecs], [rt_], bias=V("b_pw1", 16 + c))
                stt(ub[:, 30 + a:30 + bp], psA[:, 0:npc], V("b_pw1", c), t_[:, 0:npc], ALU.add, ALU.mult,
                    [rpA, rt_, r_vecs], [r_ub])
                if i == 0:
                    ts(ub[:, 30:30 + HALO], ub[:, 30:30 + HALO], V("hm"), None, ALU.mult, None, [r_ub, r_vecs], [r_ub])
                if b > NPR:
                    stt(us, psA[:, npc:npc + NS], V("b_pw1", c), t_[:, npc:npc + NS], ALU.add, ALU.mult,
                        [rpA, rt_, r_vecs], [r_small])
                    stt(cpo[:, c, :], psA[:, npc - 30:npc], V("b_pw1", c), t_[:, npc - 30:npc], ALU.add, ALU.mult,
                        [rpA, rt_, r_vecs], [r_cpo])
            for i, (a, b) in enumerate(TILES):
                bp = min(b, NPR)
                npc = bp - a
                psY, rpY = new_ps()
                for kx in range(31):
                    mm(psY[:, 0:npc], dg[:, kx, :], ub[:, a + kx:a + kx + npc], kx == 0, kx == 30,
                       [r_dg, r_ub], [rpY])
                act(ybf[:, c, a:bp], psY[:, 0:npc], AF.Identity, [rpY, r_vecs], [r_y[i]], bias=V("b_dw", c))
            tt(cso, cs, bcast(V("w_dw", c * 31, 30), 1, NS), ALU.mult, [r_cs, r_vecs], [r_cso])
            ys = small[:, 360:360 + NS]
            S.op("dve", lambda e, ys=ys: e.tensor_reduce(out=ys, in_=cso, axis=AX.X, op=ALU.add),
                 reads=[r_cso], writes=[r_small])
            stt(ys, us, V("w_dw", c * 31 + 30), ys, ALU.mult, ALU.add, [r_small, r_vecs], [r_small])
            ts(ybf[:, c, NPR:T], ys, V("b_dw", c), None, ALU.add, None, [r_small, r_vecs], [r_y[2]])
            S.op("dve", lambda e: e.tensor_copy(out=cso[:, :, 0:29], in_=cs[:, :, 1:30]), reads=[r_cs, r_cso], writes=[r_cso])
            S.op("dve", lambda e, us=us: e.tensor_copy(out=cso[:, :, 29], in_=us), reads=[r_small, r_cso], writes=[r_cso])
            sp_dma(cso_v[:, c], cso, reads=[r_cso])
        sp_dma(conv_o.rearrange("(c p) k -> p c k", p=128), cpo[:, :, :], reads=[r_cpo])
        r_z = [alias(r_hc) for _ in range(3)]
        sqL = arC[:, 0:KC * 358 // 2].bitcast(BF16).rearrange("p (c t) -> p c t", c=KC)
        r_sqL = alias([r_ub, r_cs, r_cso])
        mu, r_mu = gbuf[0], alias([r_gbuf[0]])
        rs_, r_rs = gbuf[1], alias([r_gbuf[1]])
        for i, (a, b) in enumerate(TILES):
            n = b - a
            act(sqL[:, :, 0:n], ybf[:, :, a:b], AF.Square, [r_y[i]], [r_sqL])
            psM, rpM = new_ps((4, 5))
            sum_bcast(ybf[:, :, a:b], KC, n, [r_y[i]], psM, rpM)
            psQ, rpQ = new_ps((4, 5))
            sum_bcast(sqL, KC, n, [r_sqL], psQ, rpQ)
            act(mu[:, 0:n], psM[:, 0:n], AF.Copy, [rpM], [r_mu], scale=1.0 / D)
            t_, rt_ = new_tmp()
            act(t_[:, 0:n], psM[:, 0:n], AF.Square, [rpM], [rt_], scale=1.0 / D)
            stt(rs_[:, 0:n], psQ[:, 0:n], 1.0 / D, t_[:, 0:n], ALU.mult, ALU.subtract, [rpQ, rt_], [r_rs])
            act(rs_[:, 0:n], rs_[:, 0:n], AF.Sqrt, [r_rs, r_vecs], [r_rs], bias=V("eps"))
            recip(rs_[:, 0:n], rs_[:, 0:n], [r_rs], [r_rs])
            for c in range(KC):
                t_, rt_ = new_tmp()
                tt(t_[:, 0:n], ybf[:, c, a:b], mu[:, 0:n], ALU.subtract, [r_y[i], r_mu], [rt_])
                tt(t_[:, 0:n], t_[:, 0:n], rs_[:, 0:n], ALU.mult, [rt_, r_rs], [rt_])
                act(h3[:, c, a:b], t_[:, 0:n], AF.Silu, [rt_, r_vecs], [r_z[i]], bias=V("ln_b", c), scale=V("ln_g", c))
        w2_v = w_pw2.rearrange("(c p) n -> p c n", p=128)
        for t in range(8):
            wt, rw = wload(w2_v[:, :, t * 256:(t + 1) * 256], [128, KC, 256])
            for m in range(2):
                oc = 2 * t + m
                for i, (a, b) in enumerate(TILES):
                    n = b - a
                    ps, rps = new_ps()
                    for kc in range(KC):
                        mm(ps[:, 0:n], wt[:, kc, m * 128:(m + 1) * 128], h3[:, kc, a:b], kc == 0, kc == KC - 1,
                           rw + [r_z[i]], [rps])
                    x_update(oc, i, ps, rps, 2, bias_col=V("b_pw2", oc))
        r_gbuf[0] = alias([r_mu])
        r_gbuf[1] = alias([r_rs])
        prev_h, prev_A = ffn(1, r_z, r_y)

        ystg = arA[:, 0:2 * KC * 358].bitcast(F32).rearrange("p (c t) -> p c t", c=KC)
        r_ystg = alias(prev_A)
        sqE = arA[:, 2 * KC * 358:3 * KC * 358].rearrange("p (c t) -> p c t", c=KC)
        r_sqE = alias(prev_A)
        yv = yT.rearrange("(c p) t -> p c t", p=128)
        for i, (a, b) in enumerate(TILES):
            n = b - a
            act(sqE[:, :, 0:n], xres[:, :, a:b], AF.Square, [r_xres[i]], [r_sqE])
            ps, rps = new_ps((4, 5))
            sum_bcast(sqE, KC, n, [r_sqE], ps, rps)
            rt, rr = rsqrt_from_ps(ps, rps, n, 1.0 / D)
            for kc in range(KC):
                stt(ystg[:, kc, 0:n], xres[:, kc, a:b], V("g_fin", kc), rt[:, 0:n], ALU.mult, ALU.mult,
                    [r_xres[i], rr, r_vecs], [r_ystg])
            lo, hi = max(a, HALO), min(b, NPR)
            sp_dma(yv[:, :, lo - HALO:hi - HALO], ystg[:, :, lo - a:hi - a], reads=[r_ystg])
            if b > NPR:
                sp_dma(ysT.rearrange("(c p) s -> p c s", p=128), ystg[:, :, NPR - a:T - a], reads=[r_ystg])
        S.emit()
    return nc


def _rope_tables(half):
    inv = (10000.0 ** (-np.arange(32, dtype=np.float32) / np.float32(32))).astype(np.float32)
    pos = np.empty(CKW, np.float32)
    pos[:NKEY] = np.arange(NKEY, dtype=np.float32) + np.float32(half * 1024 - 1024)
    pos[NKEY:] = np.float32(8192.0)
    ang = (pos[None, :] * inv[:, None]).astype(np.float32)
    cos, sin = np.cos(ang).astype(np.float32), np.sin(ang).astype(np.float32)
    tab = np.empty((64, 2, CKW), np.float32)
    tab[:32, 0] = cos
    tab[32:, 0] = cos
    tab[:32, 1] = -sin
    tab[32:, 1] = sin
    return tab


def _fm(v):
    return np.ascontiguousarray(np.asarray(v, np.float32).reshape(-1, 128).T)


def prepare(inp, cores):
    f = lambda k: np.asarray(inp[k], np.float32)
    x_prompt, x_sample = f("x_prompt"), f("x_sample")
    shared = {}
    shared["cache_ckv"] = f("cache_ckv").reshape(NPOOL * 32, 4 * 512)
    shared["cache_kpe"] = f("cache_kpe").reshape(NPOOL * 32, 4 * 64)
    shared["w_mod"] = f("w_mod")
    shared["wq_a"] = f("wq_a")[0]
    wqb = f("wq_b")[0].reshape(512, NH, 192)
    shared["wq_b"] = np.ascontiguousarray(np.concatenate(
        [wqb, wqb[:, :, 160:192], wqb[:, :, 128:160]], axis=2).reshape(512, NH * 256))
    wkv = f("wkv_a")[0]
    shared["wkv_a"] = np.ascontiguousarray(np.concatenate([wkv, wkv[:, 544:576], wkv[:, 512:544]], axis=1))
    wuk = f("w_uk")[0]
    shared["w_uk"] = wuk.reshape(512, NH * 128)
    shared["w_ukT"] = np.ascontiguousarray(wuk.transpose(2, 1, 0).reshape(128, NH * 512))
    shared["w_uv"] = f("w_uv")[0].reshape(512, NH * 128)
    shared["w_o"] = f("w_o")[0]
    shared["w_pw1"] = f("conv_w_pw1")[0]
    shared["w_pw2"] = f("conv_w_pw2")[0]
    shared["w_gate"] = f("ffn_w_gate")
    shared["w_up"] = f("ffn_w_up")
    shared["w_down"] = f("ffn_w_down")
    tri = np.zeros((128, 128 + HALO), np.float32)
    k = np.arange(128)[:, None]
    tri[:, :128] = (np.arange(128)[None, :] >= k)
    tri[:, 128:] = ((896 + k) <= (FAR + np.arange(HALO)[None, :]))
    shared["tri"] = tri
    shared["ident"] = np.eye(128, dtype=np.float32)
    vbase = np.zeros((128, NV), np.float32)

    def put(name, arr, off=0):
        arr = np.asarray(arr, np.float32)
        vbase[:, VC[name] + off:VC[name] + off + arr.shape[1]] = arr
    for l in range(2):
        put("b_mod", _fm(f("b_mod")[l]), l * 96)
        put("g_mix", _fm(f("norm_mix_g")[l]), l * 16)
        put("g_ffn", _fm(f("norm_ffn_g")[l]), l * 16)
        wfc = f("ffn_w_conv")[l]
        put("w_fc", wfc.reshape(3, NJ, 128).transpose(2, 1, 0).reshape(128, NJ * 3), l * NJ * 3)
        put("b_fc", _fm(f("ffn_b_conv")[l]), l * NJ)
    put("g_fin", _fm(f("final_norm_g")))
    put("g_q", _fm(f("q_norm_g")[0]))
    put("g_kv", _fm(f("kv_norm_g")[0]))
    put("b_pw1", _fm(f("conv_b_pw1")[0]))
    wdw = f("conv_w_dw")[0]
    put("w_dw", wdw.reshape(31, KC, 128).transpose(2, 1, 0).reshape(128, KC * 31))
    put("b_dw", _fm(f("conv_b_dw")[0]))
    put("ln_g", _fm(f("conv_ln_g")[0]))
    put("ln_b", _fm(f("conv_ln_b")[0]))
    put("b_pw2", _fm(f("conv_b_pw2")[0]))
    vbase[:, VC["pm16"]] = np.arange(128) % 32
    vbase[:, VC["eps"]] = 1e-6
    vbase[:, VC["tiny"]] = 1e-30
    pt = np.asarray(inp["page_table"], np.int32)
    c_prompt, c_sample = f("c_prompt"), f("c_sample")
    sconv, sffn = f("state_conv")[0], f("state_ffn")
    in_maps = []
    for c in cores:
        b, half = c // 2, c % 2
        m = dict(shared)
        xT = np.zeros((D, NKEY), np.float32)
        if half == 1:
            xT[:, :1024] = x_prompt[b, :1024].T
        xT[:, 1024:] = x_prompt[b, half * 1024:(half + 1) * 1024].T
        m["xT"] = xT
        ss = slice(c * NS, (c + 1) * NS)
        m["xsT"] = np.ascontiguousarray(x_sample[ss, 0].T)
        m["cT"] = np.ascontiguousarray(np.concatenate([c_prompt[b:b + 1], c_sample[ss]], axis=0).T)
        v = vbase.copy()
        v[:, VC["hm"]] = float(half)
        v[:, VC["cb"]] = 0.0 if half == 1 else -30000.0
        m["vecs"] = v
        m["rope"] = _rope_tables(half)
        ptc = pt[ss]
        m["ptrep"] = np.ascontiguousarray(
            ptc.reshape(NS, 16, 4)[:, :, :, None].repeat(32, axis=3).reshape(NS * 16, 128).T).astype(np.int32)
        m["state_conv"] = np.ascontiguousarray(sconv[ss].transpose(2, 0, 1))
        m["state_ffn"] = np.ascontiguousarray(sffn[:, ss].transpose(0, 3, 1, 2))
        in_maps.append(m)
    return in_maps


def assemble(res, cores, out=None):
    B, SEQ, DEC = 4, 2048, 128
    if out is None:
        out = [np.zeros((B, SEQ, D), np.float32), np.zeros((DEC, 1, D), np.float32),
               np.zeros((1, B, SEQ, 512), np.float32), np.zeros((1, B, SEQ, 64), np.float32),
               np.zeros((1, DEC, 1, 512), np.float32), np.zeros((1, DEC, 1, 64), np.float32),
               np.zeros((1, B, 30, D), np.float32), np.zeros((1, DEC, 30, D), np.float32),
               np.zeros((2, B, 2, DFF), np.float32), np.zeros((2, DEC, 2, DFF), np.float32)]
    for r, c in zip(res, cores):
        b, half = c // 2, c % 2
        ts_ = slice(half * 1024, (half + 1) * 1024)
        ss = slice(c * NS, (c + 1) * NS)
        out[0][b, ts_] = r["yT"].T
        out[1][ss, 0] = r["ysT"].T
        out[2][0, b, ts_] = r["ckv_o"].T
        out[3][0, b, ts_] = r["kpe_o"].T
        out[4][0, ss, 0] = r["ckv_so"].T
        out[5][0, ss, 0] = r["kpe_so"].T
        if half == 1:
            out[6][0, b] = r["conv_o"].T
            out[8][:, b] = r["ffn_o"].transpose(0, 2, 1)
        out[7][0, ss] = r["conv_so"].transpose(1, 2, 0)
        out[9][:, ss] = r["ffn_so"].transpose(0, 2, 3, 1)
    return tuple(out)


def kernel(**inputs):
    cores = list(range(8))
    nc = build_nc()
    in_maps = prepare(inputs, cores)
    res = run_bass_kernel_spmd(nc, in_maps, core_ids=cores)
    return assemble(res.results, cores)
```
